# Optimizing a Trainium2 kernel written in Bass

```python
import math
import jax, jax.numpy as jnp
from jax import lax
import numpy as np

D_MODEL = 2048
BATCH = 8
SEQ = 2048
DEPTH = 2

CHUNK = 64
N_A_LAYERS = DEPTH // 2
N_B_LAYERS = DEPTH - N_A_LAYERS
M_HEADS = 8
M_QK_DIM = D_MODEL // (2 * M_HEADS)
M_V_DIM = D_MODEL // M_HEADS
M_CONV = 4
M_QK_WIDTH = M_HEADS * M_QK_DIM
M_V_WIDTH = M_HEADS * M_V_DIM
M_IN_WIDTH = 2 * M_QK_WIDTH + 2 * M_V_WIDTH + 2 * M_HEADS
A_HEADS = 16
NOPE_DIM = 128
ROPE_DIM = 64
V_DIM = 128
KV_RANK = D_MODEL // 4
Q_RANK = D_MODEL // 4
ROPE_THETA = 10000.0
Q_BLOCK = 128
D_FF = 4 * D_MODEL
DN_ALPHA = (2 * DEPTH) ** 0.25
DN_BETA = (8 * DEPTH) ** -0.25
LN_EPS = 1e-5
RMS_EPS = 1e-6

kernel_name = "yoco_mlstm_mla_deepnorm_trunk"

F32 = jnp.float32


def layer_norm(x, g, b):
    xf = x.astype(F32)
    mu = jnp.mean(xf, axis=-1, keepdims=True)
    var = jnp.mean(jnp.square(xf - mu), axis=-1, keepdims=True)
    return ((xf - mu) * lax.rsqrt(var + LN_EPS) * g.astype(F32) + b.astype(F32)).astype(x.dtype)


def rms_norm(x, g):
    xf = x.astype(F32)
    ms = jnp.mean(jnp.square(xf), axis=-1, keepdims=True)
    return (xf * lax.rsqrt(ms + RMS_EPS) * g.astype(F32)).astype(x.dtype)


def rope(x, cos, sin):
    half = x.shape[-1] // 2
    x1, x2 = x[..., :half], x[..., half:]
    return jnp.concatenate([x1 * cos - x2 * sin, x2 * cos + x1 * sin], axis=-1)


def causal_depthwise_conv(x, w, b):
    c = x.shape[-1]
    y = lax.conv_general_dilated(
        x, w[:, None, :].astype(x.dtype), window_strides=(1,), padding=[(M_CONV - 1, 0)],
        dimension_numbers=("NWC", "WIO", "NWC"), feature_group_count=c)
    return y + b.astype(x.dtype)


def mlstm_mixer(x, w_in, b_gates, conv_w, conv_b, norm_w, w_out):
    B, S, _ = x.shape
    NC = S // CHUNK
    proj = x @ w_in
    qk = proj[..., :2 * M_QK_WIDTH]
    v = proj[..., 2 * M_QK_WIDTH:2 * M_QK_WIDTH + M_V_WIDTH]
    o = proj[..., 2 * M_QK_WIDTH + M_V_WIDTH:2 * M_QK_WIDTH + 2 * M_V_WIDTH]
    gates = proj[..., 2 * M_QK_WIDTH + 2 * M_V_WIDTH:].astype(F32) + b_gates.astype(F32)
    qk = jax.nn.silu(causal_depthwise_conv(qk, conv_w, conv_b))
    q = qk[..., :M_QK_WIDTH] * (M_QK_DIM ** -0.5)
    k = qk[..., M_QK_WIDTH:]
    i_pre = gates[..., :M_HEADS]
    log_f = jax.nn.log_sigmoid(gates[..., M_HEADS:])

    def to_chunks(t, d):
        return t.astype(F32).reshape(B, NC, CHUNK, M_HEADS, d).transpose(1, 0, 3, 2, 4)

    def gate_chunks(t):
        return t.reshape(B, NC, CHUNK, M_HEADS).transpose(1, 0, 3, 2)

    causal = jnp.tril(jnp.ones((CHUNK, CHUNK), dtype=bool))

    def step(carry, inp):
        C, n, m = carry
        qc, kc, vc, ic, fc = inp
        bcum = jnp.cumsum(fc, axis=-1)
        g = bcum[..., -1]
        dmat = bcum[..., :, None] - bcum[..., None, :] + ic[..., None, :]
        dmat = jnp.where(causal, dmat, -jnp.inf)
        inter = bcum + m[..., None]
        m_t = jnp.maximum(jnp.max(dmat, axis=-1), inter)
        w_intra = jnp.exp(dmat - m_t[..., None])
        w_inter = jnp.exp(inter - m_t)
        sw = jnp.einsum('bhtd,bhsd->bhts', qc, kc) * w_intra
        num = jnp.einsum('bhts,bhsv->bhtv', sw, vc) \
            + w_inter[..., None] * jnp.einsum('bhtd,bhdv->bhtv', qc, C)
        den = jnp.sum(sw, axis=-1) + w_inter * jnp.einsum('bhtd,bhd->bht', qc, n)
        h = num / jnp.maximum(jnp.abs(den), jnp.exp(-m_t))[..., None]
        a = g[..., None] - bcum + ic
        m_new = jnp.maximum(g + m, jnp.max(a, axis=-1))
        decay = jnp.exp(g + m - m_new)
        wk = jnp.exp(a - m_new[..., None])
        C_new = decay[..., None, None] * C + jnp.einsum('bhs,bhsd,bhsv->bhdv', wk, kc, vc)
        n_new = decay[..., None] * n + jnp.einsum('bhs,bhsd->bhd', wk, kc)
        return (C_new, n_new, m_new), h

    init = (jnp.zeros((B, M_HEADS, M_QK_DIM, M_V_DIM), F32),
            jnp.zeros((B, M_HEADS, M_QK_DIM), F32),
            jnp.zeros((B, M_HEADS), F32))
    xs = (to_chunks(q, M_QK_DIM), to_chunks(k, M_QK_DIM), to_chunks(v, M_V_DIM),
          gate_chunks(i_pre), gate_chunks(log_f))
    _, h = lax.scan(step, init, xs)
    h = h.transpose(1, 0, 3, 2, 4).reshape(B, S, M_HEADS, M_V_DIM)
    mu = jnp.mean(h, axis=-1, keepdims=True)
    var = jnp.mean(jnp.square(h - mu), axis=-1, keepdims=True)
    hn = (h - mu) * lax.rsqrt(var + LN_EPS) * norm_w.astype(F32).reshape(M_HEADS, M_V_DIM)
    hn = hn.reshape(B, S, M_V_WIDTH).astype(x.dtype)
    return (jax.nn.sigmoid(o) * hn) @ w_out


def shared_latent_kv(x, w_down, norm_w, w_up, cos, sin):
    B, S, _ = x.shape
    ckv = x @ w_down
    c = rms_norm(ckv[..., :KV_RANK], norm_w)
    k_rope = rope(ckv[..., KV_RANK:], cos, sin)
    kv = (c @ w_up).reshape(B, S, A_HEADS, NOPE_DIM + V_DIM)
    return kv[..., :NOPE_DIM], k_rope, kv[..., NOPE_DIM:]


def mla_mixer(x, k_nope, k_rope, v, w_dq, q_norm_w, w_uq, w_out, cos, sin):
    B, S, _ = x.shape
    q = (rms_norm(x @ w_dq, q_norm_w) @ w_uq).reshape(B, S, A_HEADS, NOPE_DIM + ROPE_DIM)
    q_nope = q[..., :NOPE_DIM]
    q_rope = rope(q[..., NOPE_DIM:], cos[:, None, :], sin[:, None, :])
    scale = (NOPE_DIM + ROPE_DIM) ** -0.5
    chunk_id = jnp.arange(S) // CHUNK
    outs = []
    for blk in range(S // Q_BLOCK):
        qs, qe = blk * Q_BLOCK, (blk + 1) * Q_BLOCK
        s = (jnp.einsum('bqhd,bkhd->bhqk', q_nope[:, qs:qe], k_nope[:, :qe])
             + jnp.einsum('bqhr,bkr->bhqk', q_rope[:, qs:qe], k_rope[:, :qe])).astype(F32) * scale
        mask = chunk_id[qs:qe, None] >= chunk_id[None, :qe]
        p = jax.nn.softmax(jnp.where(mask, s, -jnp.inf), axis=-1).astype(v.dtype)
        outs.append(jnp.einsum('bhqk,bkhv->bqhv', p, v[:, :qe]))
    o = jnp.concatenate(outs, axis=1).reshape(B, S, A_HEADS * V_DIM)
    return o @ w_out


def squared_relu_mlp(x, w1, w2):
    return jnp.square(jax.nn.relu(x @ w1)) @ w2


def setup_inputs(seed: int = 0) -> dict:
    key = jax.random.key(seed)
    ks = jax.random.split(key, 21)

    def nrm(k, shape, scale):
        return jax.random.normal(k, shape, F32) * scale

    x = nrm(ks[0], (BATCH, SEQ, D_MODEL), 1.0)
    a_w_in = nrm(ks[1], (N_A_LAYERS, D_MODEL, M_IN_WIDTH), D_MODEL ** -0.5)
    a_b_gates = jnp.concatenate([nrm(ks[2], (N_A_LAYERS, M_HEADS), 0.1),
                                 3.0 + nrm(ks[3], (N_A_LAYERS, M_HEADS), 0.5)], axis=-1)
    a_conv_w = nrm(ks[4], (N_A_LAYERS, M_CONV, 2 * M_QK_WIDTH), M_CONV ** -0.5)
    a_conv_b = nrm(ks[5], (N_A_LAYERS, 2 * M_QK_WIDTH), 0.02)
    a_norm_w = 1.0 + nrm(ks[6], (N_A_LAYERS, M_V_WIDTH), 0.02)
    a_w_out = nrm(ks[7], (N_A_LAYERS, M_V_WIDTH, D_MODEL), DN_BETA * M_V_WIDTH ** -0.5)
    kv_w_down = nrm(ks[8], (D_MODEL, KV_RANK + ROPE_DIM), D_MODEL ** -0.5)
    kv_norm_w = 1.0 + nrm(ks[9], (KV_RANK,), 0.02)
    kv_w_up = nrm(ks[10], (KV_RANK, A_HEADS * (NOPE_DIM + V_DIM)), KV_RANK ** -0.5)
    b_w_dq = nrm(ks[11], (N_B_LAYERS, D_MODEL, Q_RANK), D_MODEL ** -0.5)
    b_q_norm_w = 1.0 + nrm(ks[12], (N_B_LAYERS, Q_RANK), 0.02)
    b_w_uq = nrm(ks[13], (N_B_LAYERS, Q_RANK, A_HEADS * (NOPE_DIM + ROPE_DIM)), Q_RANK ** -0.5)
    b_w_out = nrm(ks[14], (N_B_LAYERS, A_HEADS * V_DIM, D_MODEL), DN_BETA * (A_HEADS * V_DIM) ** -0.5)
    mlp_w1 = nrm(ks[15], (DEPTH, D_MODEL, D_FF), D_MODEL ** -0.5)
    mlp_w2 = nrm(ks[16], (DEPTH, D_FF, D_MODEL), DN_BETA * D_FF ** -0.5)
    ln1_g = 1.0 + nrm(ks[17], (DEPTH, D_MODEL), 0.02)
    ln1_b = nrm(ks[18], (DEPTH, D_MODEL), 0.02)
    ln2_g = 1.0 + nrm(ks[19], (DEPTH, D_MODEL), 0.02)
    ln2_b = nrm(ks[20], (DEPTH, D_MODEL), 0.02)
    return {"x": x, "a_w_in": a_w_in, "a_b_gates": a_b_gates, "a_conv_w": a_conv_w,
            "a_conv_b": a_conv_b, "a_norm_w": a_norm_w, "a_w_out": a_w_out,
            "kv_w_down": kv_w_down, "kv_norm_w": kv_norm_w, "kv_w_up": kv_w_up,
            "b_w_dq": b_w_dq, "b_q_norm_w": b_q_norm_w, "b_w_uq": b_w_uq, "b_w_out": b_w_out,
            "mlp_w1": mlp_w1, "mlp_w2": mlp_w2,
            "ln1_g": ln1_g, "ln1_b": ln1_b, "ln2_g": ln2_g, "ln2_b": ln2_b}


def reference(x, a_w_in, a_b_gates, a_conv_w, a_conv_b, a_norm_w, a_w_out,
              kv_w_down, kv_norm_w, kv_w_up, b_w_dq, b_q_norm_w, b_w_uq, b_w_out,
              mlp_w1, mlp_w2, ln1_g, ln1_b, ln2_g, ln2_b):
    S = x.shape[1]
    pos = jnp.arange(S, dtype=F32)
    inv_freq = ROPE_THETA ** (-jnp.arange(0, ROPE_DIM, 2, dtype=F32) / ROPE_DIM)
    ang = pos[:, None] * inv_freq[None, :]
    cos = jnp.cos(ang).astype(x.dtype)
    sin = jnp.sin(ang).astype(x.dtype)
    k_nope = k_rope = v_shared = None
    for layer in range(DEPTH):
        if layer < N_A_LAYERS:
            mix = mlstm_mixer(x, a_w_in[layer], a_b_gates[layer], a_conv_w[layer],
                              a_conv_b[layer], a_norm_w[layer], a_w_out[layer])
        else:
            if layer == N_A_LAYERS:
                k_nope, k_rope, v_shared = shared_latent_kv(x, kv_w_down, kv_norm_w, kv_w_up, cos, sin)
            j = layer - N_A_LAYERS
            mix = mla_mixer(x, k_nope, k_rope, v_shared, b_w_dq[j], b_q_norm_w[j],
                            b_w_uq[j], b_w_out[j], cos, sin)
        x = layer_norm(DN_ALPHA * x + mix, ln1_g[layer], ln1_b[layer])
        x = layer_norm(DN_ALPHA * x + squared_relu_mlp(x, mlp_w1[layer], mlp_w2[layer]),
                       ln2_g[layer], ln2_b[layer])
    return x
```

```python
import contextlib
import numpy as np
import concourse.bass as bass
import concourse.mybir as mybir
from concourse.bass_utils import run_bass_kernel_spmd

F32 = mybir.dt.float32
BF16 = mybir.dt.bfloat16
AF = mybir.ActivationFunctionType
ALU = mybir.AluOpType
AX = mybir.AxisListType

S = 2048
D = 2048
DFF = 8192
NCORES = 8
DN_ALPHA = 4.0 ** 0.25
LN_EPS = 1e-5
RMS_EPS = 1e-6
SEG = 12000
NDMA = 20
NDMA_SP = 14


class Tok:
    __slots__ = ("eng", "idx", "dsem", "dval")

    def __init__(self, eng, idx, dsem=None, dval=None):
        self.eng, self.idx, self.dsem, self.dval = eng, idx, dsem, dval


class Buf:
    __slots__ = ("w", "r")

    def __init__(self):
        self.w = []
        self.r = []


class Prog:
    ENG = ("pe", "act", "dve", "pool", "sp")

    def __init__(self, nc):
        self.nc = nc
        self.ops = {e: [] for e in self.ENG}
        self.dma_cnt = [0] * NDMA
        self.dma_last = [None] * NDMA
        self.dma_rr = 0
        self.dma_rr_pool = 0

    def _deps(self, reads, writes, extra, par=False):
        deps = []
        for b in reads:
            deps.extend(b.w)
        for b in writes:
            if par:
                deps.extend(t for t in b.w if t.dsem is None)
            else:
                deps.extend(b.w)
            deps.extend(b.r)
        for t in extra:
            if t is not None:
                deps.append(t)
        return deps

    def _update(self, tok, reads, writes, inorder, par=False):
        for b in reads:
            if inorder:
                b.r = [t for t in b.r if not (t.dsem is None and t.eng == tok.eng)]
            b.r.append(tok)
        for b in writes:
            if par:
                b.w = [t for t in b.w if t.dsem is not None] + [tok]
            else:
                b.w = [tok]
            b.r = []

    def op(self, eng, fn, reads=(), writes=(), extra=()):
        deps = self._deps(reads, writes, extra)
        idx = len(self.ops[eng])
        self.ops[eng].append({"fn": fn, "deps": deps, "marked": False, "dma": None})
        tok = Tok(eng, idx)
        self._update(tok, reads, writes, True)
        return tok

    def dma(self, eng, fn, reads=(), writes=(), extra=(), par=False):
        deps = self._deps(reads, writes, extra, par)
        if eng == "pool":
            i = NDMA_SP + self.dma_rr_pool
            self.dma_rr_pool = (self.dma_rr_pool + 1) % (NDMA - NDMA_SP)
        else:
            i = self.dma_rr
            self.dma_rr = (self.dma_rr + 1) % NDMA_SP
        if self.dma_last[i] is not None:
            deps.append(self.dma_last[i])
        self.dma_cnt[i] += 16
        idx = len(self.ops[eng])
        tok = Tok(eng, idx, i, self.dma_cnt[i])
        self.dma_last[i] = tok
        self.ops[eng].append({"fn": fn, "deps": deps, "marked": False, "dma": tok})
        self._update(tok, reads, writes, False, par)
        return tok

    def barrier(self):
        last = []
        for e in self.ENG:
            for i in range(len(self.ops[e]) - 1, -1, -1):
                o = self.ops[e][i]
                if o["fn"] is not None and o["dma"] is None:
                    last.append(Tok(e, i))
                    break
        pend = [t for t in self.dma_last if t is not None]
        for e in self.ENG:
            deps = [t for t in last if t.eng != e] + pend
            self.ops[e].append({"fn": None, "deps": deps, "marked": False, "dma": None})

    def emit(self, block, es):
        nc = self.nc
        for e in self.ENG:
            for o in self.ops[e]:
                for t in o["deps"]:
                    if t.dsem is None and not (t.eng == "pe" and e == "pe"):
                        self.ops[t.eng][t.idx]["marked"] = True
        for e in self.ENG:
            for i, o in enumerate(self.ops[e]):
                assert not (o["marked"] and o["fn"] is None and o["dma"] is None) or True
        cnt = {}
        nmarks = {}
        for e in self.ENG:
            c = 0
            arr = []
            for o in self.ops[e]:
                if o["marked"] and o["dma"] is None and o["fn"] is not None:
                    c += 1
                arr.append(c)
            cnt[e] = arr
            nmarks[e] = c
        esems = {e: [es.enter_context(nc.semaphore("s_%s_%d" % (e, k))) for k in range(max(1, -(-nmarks[e] // SEG)))]
                 for e in self.ENG}
        dsems = [es.enter_context(nc.semaphore("d_%d" % i)) for i in range(NDMA)]
        handles = {"pe": block.tensor, "act": block.scalar, "dve": block.vector,
                   "pool": block.gpsimd, "sp": block.sync}
        final = [t for t in self.dma_last if t is not None]
        for e in self.ENG:
            ops = self.ops[e]
            plan = []
            seen_e = {}
            seen_d = {}
            ops2 = list(ops)
            if e == "sp":
                ops2 = ops2 + [{"fn": None, "deps": final, "marked": False, "dma": None}]
            for i, o in enumerate(ops2):
                waits = []
                for t in o["deps"]:
                    if t.dsem is not None:
                        if seen_d.get(t.dsem, 0) >= t.dval:
                            continue
                        seen_d[t.dsem] = t.dval
                        waits.append((dsems[t.dsem], t.dval))
                    else:
                        m = cnt[t.eng][t.idx]
                        if m == 0:
                            continue
                        if t.eng == e and e == "pe":
                            continue
                        if seen_e.get(t.eng, 0) >= m:
                            continue
                        seen_e[t.eng] = m
                        k = (m - 1) // SEG
                        waits.append((esems[t.eng][k], (m - 1) % SEG + 1))
                inc = None
                if o["dma"] is not None:
                    inc = (dsems[o["dma"].dsem], 16)
                elif o["marked"] and o["fn"] is not None:
                    m = cnt[e][i]
                    inc = (esems[e][(m - 1) // SEG], 1)
                plan.append((waits, o["fn"], inc))

            def body(eng, plan=plan):
                for (waits, fn, inc) in plan:
                    for (s, v) in waits:
                        eng.wait_ge(s, v)
                    if fn is not None:
                        inst = fn(eng)
                        if inc is not None:
                            inst.then_inc(inc[0], inc[1])
            handles[e](body)


class Ctx:
    def __init__(self, nc, es):
        self.nc, self.es = nc, es
        self.P = Prog(nc)
        self.n = 0

    def sb(self, shape, dt, es=None, name=None):
        self.n += 1
        return (es or self.es).enter_context(self.nc.sbuf_tensor(name or ("t%d" % self.n), list(shape), dt))

    def ps(self, shape, dt, es=None, name=None):
        self.n += 1
        return (es or self.es).enter_context(self.nc.psum_tensor(name or ("p%d" % self.n), list(shape), dt))


def layernorm_stats(C, z, zb, stset):
    P = C.P
    stats, mv, rstd, nb, sB = stset
    for c in range(4):
        P.op("dve", lambda e, c=c: e.bn_stats(out=stats[:, c, :], in_=z[:, c * 512:(c + 1) * 512]), reads=[zb], writes=[sB])
    P.op("dve", lambda e: e.bn_aggr(out=mv[:], in_=stats[:].rearrange("p a b -> p (a b)")), reads=[sB], writes=[sB])
    P.op("act", lambda e: e.activation(out=rstd[:], in_=mv[:, 1:2], func=AF.Ln, bias=C.eps_ln[:], scale=1.0), reads=[sB, C.constB], writes=[sB])
    P.op("act", lambda e: e.activation(out=rstd[:], in_=rstd[:], func=AF.Exp, scale=-0.5), reads=[sB], writes=[sB])
    P.op("dve", lambda e: e.scalar_tensor_tensor(out=nb[:], in0=mv[:, 0:1], scalar=-1.0, in1=rstd[:], op0=ALU.mult, op1=ALU.mult),
         reads=[sB], writes=[sB])


def layernorm_apply(C, z, zb, gb, bb, gbB, stset, mul_eng="dve"):
    P = C.P
    stats, mv, rstd, nb, sB = stset
    P.op("act", lambda e: e.activation(out=z, in_=z, func=AF.Identity, bias=nb[:], scale=rstd[:]), reads=[sB], writes=[zb])
    P.op(mul_eng, lambda e: e.tensor_tensor(out=z, in0=z, in1=gb[:], op=ALU.mult), reads=[gbB], writes=[zb])
    P.op("dve", lambda e: e.tensor_tensor(out=z, in0=z, in1=bb[:], op=ALU.add), reads=[gbB], writes=[zb])


def layernorm_tiles(C, zs, zbs, gb, bb, gbB, st, store_fn, mul_eng="dve"):
    n = len(zs)
    assert len(st) >= n
    for i in range(n):
        layernorm_stats(C, zs[i], zbs[i], st[i])
    for i in range(n):
        layernorm_apply(C, zs[i], zbs[i], gb, bb, gbB, st[i], mul_eng)
        store_fn(i)


def layernorm_tiles_act(C, zs, zbs, gb, bb, gbB, st2, store_fn):
    P = C.P
    for i in range(len(zs)):
        z, zb = zs[i], zbs[i]
        s1, s2, nm, t1, var, rstd, nb, junk, sB = st2[i % len(st2)]
        P.op("act", lambda e, z=z, s1=s1: e.activation(out=z, in_=z, func=AF.Identity, accum_out=s1[:]), reads=[], writes=[zb, sB])
        P.op("act", lambda e, z=z, s2=s2, junk=junk: e.activation(out=junk[:], in_=z, func=AF.Square, accum_out=s2[:]), reads=[zb], writes=[sB])
        P.op("pool", lambda e, s1=s1, nm=nm: e.tensor_tensor(out=nm[:], in0=s1[:], in1=C.c_negmean[:], op=ALU.mult), reads=[C.constB], writes=[sB])
        P.op("pool", lambda e, s2=s2, t1=t1: e.tensor_tensor(out=t1[:], in0=s2[:], in1=C.c_invd[:], op=ALU.mult), reads=[C.constB], writes=[sB])
        P.op("pool", lambda e, nm=nm, var=var: e.tensor_tensor(out=var[:], in0=nm[:], in1=nm[:], op=ALU.mult), reads=[], writes=[sB])
        P.op("pool", lambda e, t1=t1, var=var: e.tensor_tensor(out=var[:], in0=t1[:], in1=var[:], op=ALU.subtract), reads=[], writes=[sB])
        P.op("act", lambda e, var=var, rstd=rstd: e.activation(out=rstd[:], in_=var[:], func=AF.Ln, bias=C.eps_ln[:], scale=1.0), reads=[C.constB], writes=[sB])
        P.op("act", lambda e, rstd=rstd: e.activation(out=rstd[:], in_=rstd[:], func=AF.Exp, scale=-0.5), reads=[], writes=[sB])
        P.op("pool", lambda e, nm=nm, rstd=rstd, nb=nb: e.tensor_tensor(out=nb[:], in0=nm[:], in1=rstd[:], op=ALU.mult), reads=[], writes=[sB])
        P.op("act", lambda e, z=z, nb=nb, rstd=rstd: e.activation(out=z, in_=z, func=AF.Identity, bias=nb[:], scale=rstd[:]), reads=[sB], writes=[zb])
        P.op("pool", lambda e, z=z: e.tensor_tensor(out=z, in0=z, in1=gb[:], op=ALU.mult), reads=[gbB], writes=[zb])
        P.op("pool", lambda e, z=z: e.tensor_tensor(out=z, in0=z, in1=bb[:], op=ALU.add), reads=[gbB], writes=[zb])
        store_fn(i)


def ln_act_sets(C, es, n=4):
    out = []
    for _ in range(n):
        out.append(tuple(C.sb([128, 1], F32, es=es) for _ in range(7)) + (C.sb([128, D], BF16, es=es), Buf()))
    return out


def ln_stat_sets(C, es, n=8):
    return [(C.sb([128, 4, 6], F32, es=es), C.sb([128, 2], F32, es=es), C.sb([128, 1], F32, es=es), C.sb([128, 1], F32, es=es), Buf()) for _ in range(n)]


class PanelRing:
    def __init__(self, C, n, es):
        self.C = C
        self.n = n
        self.bufs = [C.sb([128, 16 * 512], BF16, es=es) for _ in range(n)]
        self.tb = [Buf() for _ in range(n)]
        self.srcs = []
        self.issued = 0
        self.taken = 0

    def plan(self, srcs):
        self.srcs.extend(srcs)

    def _issue(self):
        k = self.issued
        t, b = self.bufs[k % self.n], self.tb[k % self.n]
        for j, (dst, src) in enumerate(self.srcs[k](t)):
            self.C.P.dma("pool", lambda e, dst=dst, src=src: e.dma_start(out=dst, in_=src), writes=[b], par=(j > 0))
        self.issued += 1

    def next(self):
        while self.issued < len(self.srcs) and self.issued < self.taken + self.n:
            if self.issued - self.n >= self.taken:
                break
            self._issue()
        k = self.taken
        self.taken += 1
        return self.bufs[k % self.n], self.tb[k % self.n]


def panel_src(W, r0, c0, ncols=512, krows=16):
    def src(t):
        return [(t[:, 0:krows * ncols].rearrange("p (k n) -> p k n", k=krows),
                 W[r0:r0 + krows * 128, c0:c0 + ncols].rearrange("(k p) n -> p k n", p=128))]
    return src


def stage_tm_srcs(W, KC, ncols):
    return [panel_src(W, q * 2048, nt * 512) for nt in range(ncols // 512) for q in range(KC // 16)]


def stage_tm(C, ring, lhsT_fn, lhsT_bufs, KC, W, ncols, xs, xsb, pbanks, pbb, alpha_res=True, wres=None, wresB=None):
    P = C.P
    NQ = KC // 16
    for nt in range(ncols // 512):
        for q in range(NQ):
            if wres is None:
                wt, wb = ring.next()
                rhs_fn = lambda fc, wt=wt: wt[:, fc * 512:(fc + 1) * 512]
            else:
                wb = wresB
                rhs_fn = lambda fc, nt=nt: wres[:, fc, nt * 512:(nt + 1) * 512]
            for sub in range(4):
                for fc in range(16):
                    kc = q * 16 + fc
                    P.op("pe", lambda e, rhs_fn=rhs_fn, fc=fc, kc=kc, sub=sub, first=(kc == 0), last=(kc == KC - 1):
                         e.matmul(out=pbanks[sub][:], lhsT=lhsT_fn(kc, sub), rhs=rhs_fn(fc), start=first, stop=last),
                         reads=[wb] + lhsT_bufs(kc, sub), writes=[pbb[sub]])
        for sub in range(4):
            zs = xs[:, sub, nt * 512:(nt + 1) * 512]
            if alpha_res:
                P.op("dve", lambda e, zs=zs, sub=sub: e.scalar_tensor_tensor(out=zs, in0=zs, scalar=DN_ALPHA, in1=pbanks[sub][:],
                                                                             op0=ALU.mult, op1=ALU.add),
                     reads=[pbb[sub]], writes=[xsb[sub]])
            else:
                P.op("act", lambda e, zs=zs, sub=sub: e.activation(out=zs, in_=pbanks[sub][:], func=AF.Copy),
                     reads=[pbb[sub]], writes=[xsb[sub]])


def load_and_transpose(C, x_dram, row0, xs, xsb, xT, xTb, ptr, ptrb, ident, nsub=4, KC=16):
    P = C.P
    for sub in range(nsub):
        P.dma("sp", lambda e, sub=sub: e.dma_start(out=xs[:, sub, :], in_=x_dram[row0 + sub * 128: row0 + (sub + 1) * 128, :]),
              writes=[xsb[sub]])
    k = 0
    for sub in range(nsub):
        for g in range(KC // 4):
            pt, ptb = ptr[k % len(ptr)], ptrb[k % len(ptr)]
            for j in range(4):
                kc = g * 4 + j
                P.op("pe", lambda e, pt=pt, j=j, kc=kc, sub=sub: e.transpose(out=pt[:, j, :], in_=xs[:, sub, kc * 128:(kc + 1) * 128],
                                                                            identity=ident[:]),
                     reads=[xsb[sub]], writes=[ptb])
            eng = "act" if k % 2 == 0 else "dve"
            dst = xT[:, g * 4:(g + 1) * 4, sub * 128:(sub + 1) * 128]
            if eng == "act":
                P.op("act", lambda e, pt=pt, dst=dst: e.activation(out=dst, in_=pt[:], func=AF.Copy), reads=[ptb], writes=[xTb])
            else:
                P.op("dve", lambda e, pt=pt, dst=dst: e.tensor_copy(out=dst, in_=pt[:]), reads=[ptb], writes=[xTb])
            k += 1


def load_consts(C, consts):
    P = C.P
    C.ident = C.sb([128, 128], F32, name="ident_sb")
    C.identB = Buf()
    P.dma("sp", lambda e: e.dma_start(out=C.ident[:], in_=consts["ident"]), writes=[C.identB])
    C.eps_ln = C.sb([128, 1], F32, name="eps_ln")
    C.eps_rms = C.sb([128, 1], F32, name="eps_rms")
    cb = Buf()
    C.c_negmean = C.sb([128, 1], F32, name="c_negmean")
    C.c_invd = C.sb([128, 1], F32, name="c_invd")
    P.op("dve", lambda e: e.memset(C.c_negmean[:], -1.0 / D), writes=[cb])
    P.op("dve", lambda e: e.memset(C.c_invd[:], 1.0 / D), writes=[cb])
    P.op("dve", lambda e: e.memset(C.eps_ln[:], LN_EPS), writes=[cb])
    P.op("dve", lambda e: e.memset(C.eps_rms[:], RMS_EPS), writes=[cb])
    C.constB = cb


def bcast_load(C, dst, vec_dram, buf):
    C.P.dma("sp", lambda e: e.dma_start(out=dst[:], in_=vec_dram.partition_broadcast(128)), writes=[buf])


def phase_mlp(C, x_in, w1, w2, g, b, x_out, xinB, xoutB):
    P = C.P
    with contextlib.ExitStack() as es:
        xs = C.sb([128, 4, D], F32, es=es)
        xT = C.sb([128, 16, 512], BF16, es=es)
        hT = C.sb([128, 64, 512], BF16, es=es)
        ring = PanelRing(C, 3, es)
        rst = [C.sb([128, 512], F32, es=es) for _ in range(2)]
        gb = C.sb([128, D], F32, es=es)
        bb = C.sb([128, D], F32, es=es)
        st = ln_stat_sets(C, es)
        pb = [C.ps([128, 512], F32, es=es) for _ in range(8)]
        pbB = [Buf() for _ in range(8)]
        gbB = Buf()
        bcast_load(C, gb, g, gbB)
        bcast_load(C, bb, b, gbB)
        xsb = [Buf() for _ in range(4)]
        xTb = Buf()
        rstB = [Buf() for _ in range(2)]
        hTb = [Buf() for _ in range(64)]
        ptr = [pb[4 + i][:].rearrange("p (a b) -> p a b", a=4) for i in range(4)]

        for tt in range(S // 512):
            ring.plan([panel_src(w1, 0, fg * 512) for fg in range(16)])
            ring.plan(stage_tm_srcs(w2, 64, D))
        for tt in range(S // 512):
            row0 = tt * 512
            for sub in range(4):
                P.dma("sp", lambda e, sub=sub, row0=row0: e.dma_start(out=xs[:, sub, :], in_=x_in[row0 + sub * 128: row0 + (sub + 1) * 128, :]),
                      reads=[xinB], writes=[xsb[sub]])
            k = 0
            for sub in range(4):
                for gq in range(4):
                    pt, ptb = ptr[k % 4], pbB[4 + k % 4]
                    for j in range(4):
                        kc = gq * 4 + j
                        P.op("pe", lambda e, pt=pt, j=j, kc=kc, sub=sub: e.transpose(out=pt[:, j, :], in_=xs[:, sub, kc * 128:(kc + 1) * 128],
                                                                                    identity=C.ident[:]),
                             reads=[xsb[sub], C.identB], writes=[ptb])
                    dst = xT[:, gq * 4:(gq + 1) * 4, sub * 128:(sub + 1) * 128]
                    if k % 2 == 0:
                        P.op("act", lambda e, pt=pt, dst=dst: e.activation(out=dst, in_=pt, func=AF.Copy), reads=[ptb], writes=[xTb])
                    else:
                        P.op("dve", lambda e, pt=pt, dst=dst: e.tensor_copy(out=dst, in_=pt), reads=[ptb], writes=[xTb])
                    k += 1
            k = 0
            for fg in range(16):
                wt, wb = ring.next()
                for fc in range(4):
                    bank, bankB = pb[4 + k % 4], pbB[4 + k % 4]
                    for kc in range(16):
                        P.op("pe", lambda e, wt=wt, kc=kc, fc=fc, bank=bank:
                             e.matmul(out=bank[:], lhsT=wt[:, kc * 512 + fc * 128: kc * 512 + (fc + 1) * 128], rhs=xT[:, kc, :],
                                      start=(kc == 0), stop=(kc == 15)),
                             reads=[wb, xTb], writes=[bankB])
                    r, rB = rst[k % 2], rstB[k % 2]
                    P.op("act", lambda e, r=r, bank=bank: e.activation(out=r[:], in_=bank[:], func=AF.Relu), reads=[bankB], writes=[rB])
                    f = fg * 4 + fc
                    P.op("dve", lambda e, r=r, f=f: e.tensor_tensor(out=hT[:, f, :], in0=r[:], in1=r[:], op=ALU.mult),
                         reads=[rB], writes=[hTb[f]])
                    k += 1
            stage_tm(C, ring, lambda kc, sub: hT[:, kc, sub * 128:(sub + 1) * 128], lambda kc, sub: [hTb[kc]], 64, w2, D,
                     xs, xsb, pb[0:4], pbB[0:4])
            def store(sub, row0=row0):
                P.dma("sp", lambda e, sub=sub, row0=row0: e.dma_start(out=x_out[row0 + sub * 128: row0 + (sub + 1) * 128, :], in_=xs[:, sub, :]),
                      reads=[xsb[sub]], writes=[xoutB], par=True)
            layernorm_tiles(C, [xs[:, sub, :] for sub in range(4)], xsb, gb, bb, gbB, st, store)
        P.barrier()


def build_xT_full(C, x_in, xinB, xT, xTb, xs, xsb, pbanks, pbB):
    P = C.P
    k = 0
    nsub = len(xsb)
    for tt in range(S // (128 * nsub)):
        row0 = tt * 128 * nsub
        for sub in range(nsub):
            P.dma("sp", lambda e, sub=sub, row0=row0: e.dma_start(out=xs[:, sub, :], in_=x_in[row0 + sub * 128: row0 + (sub + 1) * 128, :]),
                  reads=[xinB], writes=[xsb[sub]])
        for sub in range(nsub):
            for gq in range(4):
                pt = pbanks[k % len(pbanks)][:].rearrange("p (a b) -> p a b", a=4)
                ptb = pbB[k % len(pbanks)]
                for j in range(4):
                    kc = gq * 4 + j
                    P.op("pe", lambda e, pt=pt, j=j, kc=kc, sub=sub: e.transpose(out=pt[:, j, :], in_=xs[:, sub, kc * 128:(kc + 1) * 128],
                                                                                identity=C.ident[:]),
                         reads=[xsb[sub], C.identB], writes=[ptb])
                dst = xT[:, gq * 4:(gq + 1) * 4, row0 + sub * 128: row0 + (sub + 1) * 128]
                if k % 2 == 0:
                    P.op("act", lambda e, pt=pt, dst=dst: e.activation(out=dst, in_=pt, func=AF.Copy), reads=[ptb], writes=[xTb])
                else:
                    P.op("dve", lambda e, pt=pt, dst=dst: e.tensor_copy(out=dst, in_=pt), reads=[ptb], writes=[xTb])
                k += 1


def cols_src(W, r0, krows, colblocks):
    tot = sum(n for _, n in colblocks)

    def src(t):
        out = []
        view = t[:, 0:krows * tot].rearrange("p (k n) -> p k n", k=krows)
        o = 0
        for (c0, n) in colblocks:
            out.append((view[:, :, o:o + n], W[r0:r0 + krows * 128, c0:c0 + n].rearrange("(k p) n -> p k n", p=128)))
            o += n
        return out
    return src


def lin_fm(C, ring, xT, xTb, KC, width, banks, banksB, evac, ntok=S, kcnt=[0]):
    P = C.P
    wt, wb = ring.next()
    nfc = -(-width // 128)
    for fc in range(nfc):
        m = min(128, width - fc * 128)
        for tokt in range(ntok // 512):
            i = kcnt[0] % len(banks)
            kcnt[0] += 1
            bank, bankB = banks[i], banksB[i]
            for kc in range(KC):
                P.op("pe", lambda e, wt=wt, kc=kc, fc=fc, m=m, bank=bank, tokt=tokt:
                     e.matmul(out=bank[0:m, :], lhsT=wt[:, kc * width + fc * 128: kc * width + fc * 128 + m],
                              rhs=xT[:, kc, tokt * 512:(tokt + 1) * 512], start=(kc == 0), stop=(kc == KC - 1)),
                     reads=[wb, xTb], writes=[bankB])
            evac(fc, tokt, bank, bankB, m)


def lin_tm(C, ring, xT, xTb, KC, width, banks, banksB, evac, ntok=S, kcnt=[0]):
    P = C.P
    wt, wb = ring.next()
    for tokt in range(ntok // 128):
        i = kcnt[0] % len(banks)
        kcnt[0] += 1
        bank, bankB = banks[i], banksB[i]
        for kc in range(KC):
            P.op("pe", lambda e, wt=wt, kc=kc, bank=bank, tokt=tokt:
                 e.matmul(out=bank[:, 0:width], lhsT=xT[:, kc, tokt * 128:(tokt + 1) * 128], rhs=wt[:, kc * width:(kc + 1) * width],
                          start=(kc == 0), stop=(kc == KC - 1)),
                 reads=[wb, xTb], writes=[bankB])
        evac(tokt, bank, bankB)


def phase_mlstm(C, x_in, xinB, w_in, b_gates, conv_w, conv_b, norm_w, consts, v_s, so_s, hg_d, hgB):
    P = C.P
    QS = 128 ** -0.5
    vB, soB = Buf(), Buf()
    with contextlib.ExitStack() as es0:
        qT = C.sb([128, 8, S], BF16, es=es0)
        kT = C.sb([128, 8, S], BF16, es=es0)
        g_all = C.sb([128, 16, 16], F32, es=es0)
        qTb = [Buf() for _ in range(8)]
        kTb = [Buf() for _ in range(8)]
        gB = Buf()
        pb = [C.ps([128, 512], F32, es=es0) for _ in range(8)]
        pbB = [Buf() for _ in range(8)]
        with contextlib.ExitStack() as es:
            xT = C.sb([128, 16, S], BF16, es=es)
            xTb = Buf()
            xs = C.sb([128, 2, D], F32, es=es)
            xsb = [Buf() for _ in range(2)]
            ring = PanelRing(C, 2, es)
            cw = C.sb([128, 16, 5], F32, es=es)
            cwB = Buf()
            for j in range(4):
                P.dma("sp", lambda e, j=j: e.dma_start(out=cw[:, :, j], in_=conv_w[j].rearrange("(c p) -> p c", p=128), allow_slow_non_contiguous=True),
                      writes=[cwB], par=True)
            P.dma("sp", lambda e: e.dma_start(out=cw[:, :, 4], in_=conv_b.rearrange("(c p) -> p c", p=128), allow_slow_non_contiguous=True),
                  writes=[cwB], par=True)
            bgb = C.sb([128, 16], F32, es=es)
            bcast_load(C, bgb, b_gates, cwB)
            one_c = C.sb([128, 1], F32, es=es)
            P.op("dve", lambda e: e.memset(one_c[:], 1.0), writes=[cwB])
            build_xT_full(C, x_in, xinB, xT, xTb, xs, xsb, pb[4:8], pbB[4:8])
            ring.plan([panel_src(w_in, 0, p * 512) for p in range(4)])
            ring.plan([panel_src(w_in, 0, 2048 + p * 512) for p in range(8)])
            ring.plan([cols_src(w_in, 0, 16, [(6144, 16)])])
            raw = [C.sb([128, 3 + S], F32, es=es) for _ in range(2)]
            rawB = [Buf() for _ in range(2)]
            acc = [xs[:, 1, :]]
            accB = [xsb[1]]
            for r_ in range(2):
                P.op("pool", lambda e, r_=r_: e.memset(raw[r_][:, 0:3], 0.0), writes=[rawB[r_]])
            for pnl in range(4):
                def evac(fc, tokt, bank, bankB, m, pnl=pnl):
                    ci = pnl * 4 + fc
                    r, rB = raw[ci % 2], rawB[ci % 2]
                    P.op("act", lambda e, r=r, bank=bank, tokt=tokt: e.activation(out=r[:, 3 + tokt * 512: 3 + (tokt + 1) * 512], in_=bank[:],
                                                                                 func=AF.Copy), reads=[bankB], writes=[rB])
                    if tokt == 3:
                        a, aB = acc[0], accB[0]
                        P.op("dve", lambda e, a=a, r=r, ci=ci: e.tensor_scalar(out=a[:], in0=r[:, 3:3 + S], scalar1=cw[:, ci, 3:4], scalar2=cw[:, ci, 4:5],
                                                                              op0=ALU.mult, op1=ALU.add), reads=[rB, cwB], writes=[aB])
                        for j in (2, 1, 0):
                            eng = "dve"
                            P.op(eng, lambda e, a=a, r=r, ci=ci, j=j: e.scalar_tensor_tensor(out=a[:], in0=r[:, j:j + S], scalar=cw[:, ci, j:j + 1], in1=a[:],
                                                                                            op0=ALU.mult, op1=ALU.add), reads=[rB, cwB], writes=[aB])
                        if ci < 8:
                            P.op("act", lambda e, a=a: e.activation(out=a[:], in_=a[:], func=AF.Silu), reads=[], writes=[aB])
                            P.op("act", lambda e, a=a, ci=ci: e.activation(out=qT[:, ci, :], in_=a[:], func=AF.Copy, scale=QS),
                                 reads=[aB], writes=[qTb[ci]])
                        else:
                            P.op("act", lambda e, a=a, ci=ci: e.activation(out=kT[:, ci - 8, :], in_=a[:], func=AF.Silu), reads=[aB], writes=[kTb[ci - 8]])
                lin_fm(C, ring, xT, xTb, 16, 512, pb[0:4], pbB[0:4], evac)
            vst = [C.sb([128, 512], BF16, es=es) for _ in range(2)]
            vstB = [Buf() for _ in range(2)]
            ost = [C.sb([128, 512], F32, es=es) for _ in range(2)]
            ostB = [Buf() for _ in range(2)]
            cnt = [0]
            for pnl in range(8):
                def evac(tokt, bank, bankB, pnl=pnl):
                    i = cnt[0] % 2
                    cnt[0] += 1
                    if pnl < 4:
                        P.op("act", lambda e, i=i, bank=bank: e.activation(out=vst[i][:], in_=bank[:], func=AF.Copy), reads=[bankB], writes=[vstB[i]])
                        P.dma("sp", lambda e, i=i, tokt=tokt, pnl=pnl: e.dma_start(out=v_s[tokt * 128:(tokt + 1) * 128, pnl * 512:(pnl + 1) * 512], in_=vst[i][:]),
                              reads=[vstB[i]], writes=[vB], par=True)
                    else:
                        c0 = (pnl - 4) * 512
                        P.op("act", lambda e, i=i, bank=bank: e.activation(out=ost[i][:], in_=bank[:], func=AF.Sigmoid), reads=[bankB], writes=[ostB[i]])
                        P.dma("sp", lambda e, i=i, tokt=tokt, c0=c0: e.dma_start(out=so_s[tokt * 128:(tokt + 1) * 128, c0:c0 + 512], in_=ost[i][:]),
                              reads=[ostB[i]], writes=[soB], par=True)
                lin_tm(C, ring, xT, xTb, 16, 512, pb[0:4], pbB[0:4], evac)
            def evac_g(tokt, bank, bankB):
                P.op("dve", lambda e, bank=bank, tokt=tokt: e.tensor_tensor(out=g_all[:, tokt, :], in0=bank[:, 0:16], in1=bgb[:], op=ALU.add),
                     reads=[bankB, cwB], writes=[gB])
            lin_tm(C, ring, xT, xTb, 16, 16, pb[0:4], pbB[0:4], evac_g)
            ftmp = C.sb([128, 16, 8], F32, es=es)
            P.op("act", lambda e: e.activation(out=ftmp[:], in_=g_all[:, :, 8:16], func=AF.Exp, scale=-1.0), reads=[gB], writes=[gB])
            P.op("act", lambda e: e.activation(out=ftmp[:], in_=ftmp[:], func=AF.Ln, bias=one_c[:], scale=1.0), reads=[cwB], writes=[gB])
            P.op("dve", lambda e: e.tensor_scalar(out=g_all[:, :, 8:16], in0=ftmp[:], scalar1=-1.0, scalar2=None, op0=ALU.mult), reads=[], writes=[gB])
            P.barrier()
        with contextlib.ExitStack() as es:
            U = C.sb([128, 128], F32, es=es)
            LT = C.sb([128, 128], F32, es=es)
            MB = C.sb([128, 128], F32, es=es)
            ONE = C.sb([128, 128], F32, es=es)
            identb = C.sb([128, 128], BF16, es=es)
            oneb = C.sb([128, 1], BF16, es=es)
            cB = Buf()
            P.dma("sp", lambda e: e.dma_start(out=U[:], in_=consts["U"]), writes=[cB])
            P.dma("sp", lambda e: e.dma_start(out=LT[:], in_=consts["LT"]), writes=[cB])
            P.dma("sp", lambda e: e.dma_start(out=MB[:], in_=consts["MB"]), writes=[cB])
            P.op("dve", lambda e: e.memset(ONE[:], 1.0), writes=[cB])
            P.op("dve", lambda e: e.memset(oneb[:], 1.0), writes=[cB])
            P.op("dve", lambda e: e.tensor_copy(out=identb[:], in_=C.ident[:]), reads=[C.identB], writes=[cB])
            Cst = C.sb([128, 8, 257], F32, es=es)
            CB = [Buf() for _ in range(8)]
            P.op("pool", lambda e: e.memset(Cst[:], 0.0), writes=CB)
            R1 = C.sb([128, 8, 128], F32, es=es)
            R2 = C.sb([128, 8, 128], F32, es=es)
            RB = Buf()
            WT = C.sb([128, 8, 128], F32, es=es)
            EBs = C.sb([128, 8, 128], F32, es=es)
            WTB = [Buf() for _ in range(2)]
            EBB = [Buf() for _ in range(2)]
            swT = C.sb([128, 8, 128], BF16, es=es)
            swB = [Buf() for _ in range(2)]
            qs = C.sb([128, 8, 128], F32, es=es)
            qsB = [Buf() for _ in range(2)]
            kk = C.sb([128, 8, 128], BF16, es=es)
            kkB = [Buf() for _ in range(2)]
            vt = [C.sb([128, 8, 257], BF16, es=es) for _ in range(2)]
            vtB = [Buf() for _ in range(2)]
            vw = C.sb([128, 8, 257], BF16, es=es)
            vwB = [Buf() for _ in range(2)]
            sot = [C.sb([128, D], F32, es=es) for _ in range(2)]
            sotB = [Buf() for _ in range(2)]
            hh = C.sb([128, 8, 256], F32, es=es)
            hhB = [Buf() for _ in range(2)]
            hgt = [C.sb([128, D], BF16, es=es) for _ in range(2)]
            hgtB = [Buf() for _ in range(2)]
            den = C.sb([128, 8], F32, es=es)
            bst = C.sb([128, 8, 6], F32, es=es)
            mv = C.sb([128, 8, 2], F32, es=es)
            rstd = C.sb([128, 8], F32, es=es)
            nbias = C.sb([128, 8], F32, es=es)
            smB = [Buf() for _ in range(2)]
            nwb = C.sb([128, D], F32, es=es)
            nwB = Buf()
            bcast_load(C, nwb, norm_w, nwB)
            for i in range(2):
                P.op("pool", lambda e, i=i: e.memset(vt[i][:, :, 256:257], 1.0), writes=[vtB[i]])
            pD, pE, pS, pK, pN0, pN1, pC0, pC1 = pb
            bD, bE, bS, bK, bN0, bN1, bC0, bC1 = pbB
            pKb = pK[:].bitcast(BF16)
            WT2 = [WT, C.sb([128, 8, 128], F32, es=es)]
            EBs2 = [EBs, C.sb([128, 8, 128], F32, es=es)]
            swT2 = [swT, C.sb([128, 8, 128], BF16, es=es)]
            qs2 = [qs, C.sb([128, 8, 128], F32, es=es)]
            kk2 = [kk, C.sb([128, 8, 128], BF16, es=es)]
            vw2 = [vw, C.sb([128, 8, 257], BF16, es=es)]
            WTB2, EBB2, swB2, qsB2, kkB2, vwB2 = [[[Buf() for _ in range(2)] for _ in range(2)] for _ in range(6)]

            def prefetch(tt):
                c0 = tt * 128
                i2 = tt % 2
                P.dma("sp", lambda e, i2=i2, c0=c0: e.dma_start(out=vt[i2][:, :, 0:256], in_=v_s[c0:c0 + 128, :].rearrange("p (h v) -> p h v", h=8)),
                      reads=[vB], writes=[vtB[i2]])
                P.dma("sp", lambda e, i2=i2, c0=c0: e.dma_start(out=sot[i2][:], in_=so_s[c0:c0 + 128, :]), reads=[soB], writes=[sotB[i2]])

            def emit_R(tt):
                P.op("dve", lambda e, tt=tt: e.tensor_tensor(out=R1[:], in0=g_all[:, tt, 8:16].unsqueeze(2).broadcast_to([128, 8, 128]),
                                                            in1=LT[:].unsqueeze(1).broadcast_to([128, 8, 128]), op=ALU.mult), reads=[gB, cB], writes=[RB])
                P.op("pool", lambda e, tt=tt: e.tensor_tensor(out=R2[:], in0=g_all[:, tt, 0:8].unsqueeze(2).broadcast_to([128, 8, 128]),
                                                             in1=MB[:].unsqueeze(1).broadcast_to([128, 8, 128]), op=ALU.add), reads=[gB, cB], writes=[RB])

            def pre(tt, hg):
                c0 = tt * 128
                i2 = tt % 2
                par = tt % 2
                WT, EBs, swT, qs, kk, vw = WT2[par], EBs2[par], swT2[par], qs2[par], kk2[par], vw2[par]
                WTB, EBB, swB, qsB, kkB, vwB = WTB2[par], EBB2[par], swB2[par], qsB2[par], kkB2[par], vwB2[par]
                hs = slice(hg * 4, hg * 4 + 4)
                hs = slice(hg * 4, hg * 4 + 4)
                R1g = R1[:, hs, :].rearrange("p a b -> p (a b)")
                R2g = R2[:, hs, :].rearrange("p a b -> p (a b)")
                P.op("pe", lambda e, R1g=R1g: e.matmul(out=pD[:], lhsT=U[:], rhs=R1g, start=True, stop=False), reads=[RB, cB], writes=[bD])
                P.op("pe", lambda e, R2g=R2g: e.matmul(out=pD[:], lhsT=C.ident[:], rhs=R2g, start=False, stop=True), reads=[RB, C.identB], writes=[bD])
                P.op("pe", lambda e, R1g=R1g: e.matmul(out=pE[:], lhsT=ONE[:], rhs=R1g, start=True, stop=True), reads=[RB, cB], writes=[bE])
                WTg = WT[:, hs, :]
                EBg = EBs[:, hs, :]
                P.op("act", lambda e, WTg=WTg: e.activation(out=WTg.rearrange("p a b -> p (a b)"), in_=pD[:], func=AF.Exp), reads=[bD], writes=[WTB[hg]])
                P.op("act", lambda e, EBg=EBg: e.activation(out=EBg.rearrange("p a b -> p (a b)"), in_=pE[:], func=AF.Exp), reads=[bE], writes=[EBB[hg]])
                for h4 in range(4):
                    h = hg * 4 + h4
                    P.op("pe", lambda e, h=h, h4=h4, c0=c0: e.matmul(out=pS[:, h4 * 128:(h4 + 1) * 128], lhsT=kT[:, h, c0:c0 + 128], rhs=qT[:, h, c0:c0 + 128],
                                                                    start=True, stop=True), reads=[kTb[h], qTb[h]], writes=[bS])
                swg = swT[:, hs, :]
                P.op("dve", lambda e, swg=swg, WTg=WTg: e.tensor_tensor(out=swg.rearrange("p a b -> p (a b)"), in0=pS[:], in1=WTg.rearrange("p a b -> p (a b)"), op=ALU.mult),
                     reads=[bS, WTB[hg]], writes=[swB[hg]])
                qsg = qs[:, hs, :]
                P.op("dve", lambda e, qsg=qsg, EBg=EBg, hs=hs, c0=c0: e.tensor_tensor(out=qsg, in0=qT[:, hs, c0:c0 + 128], in1=EBg, op=ALU.mult),
                     reads=[EBB[hg]] + qTb[hg * 4:hg * 4 + 4], writes=[qsB[hg]])
                for h4 in range(4):
                    h = hg * 4 + h4
                    P.op("pe", lambda e, h=h, h4=h4, c0=c0: e.transpose(out=pKb[:, h4 * 128:(h4 + 1) * 128], in_=kT[:, h, c0:c0 + 128], identity=identb[:]),
                         reads=[kTb[h], cB], writes=[bK])
                kkg = kk[:, hs, :]
                P.op("act", lambda e, kkg=kkg: e.activation(out=kkg.rearrange("p a b -> p (a b)"), in_=pKb[:, 0:512], func=AF.Copy), reads=[bK], writes=[kkB[hg]])
                vwg = vw[:, hs, :]
                P.op("pool", lambda e, vwg=vwg, i2=i2, hs=hs: e.tensor_tensor(out=vwg, in0=vt[i2][:, hs, :], in1=WT[:, hs, 127:128].broadcast_to([128, 4, 257]), op=ALU.mult),
                     reads=[vtB[i2], WTB[hg]], writes=[vwB[hg]])

            def post(tt, hg):
                c0 = tt * 128
                i2 = tt % 2
                par = tt % 2
                WT, EBs, swT, qs, kk, vw = WT2[par], EBs2[par], swT2[par], qs2[par], kk2[par], vw2[par]
                WTB, EBB, swB, qsB, kkB, vwB = WTB2[par], EBB2[par], swB2[par], qsB2[par], kkB2[par], vwB2[par]
                hs = slice(hg * 4, hg * 4 + 4)
                pN = [pN0, pN1]
                bN = [bN0, bN1]
                for h4 in range(4):
                    h = hg * 4 + h4
                    o_ = pN[h4 // 2][:, (h4 % 2) * 256:(h4 % 2) * 256 + 256]
                    P.op("pe", lambda e, h=h, o_=o_, i2=i2: e.matmul(out=o_, lhsT=swT[:, h, :], rhs=vt[i2][:, h, 0:256], start=True, stop=False),
                         reads=[swB[hg], vtB[i2]], writes=[bN[h4 // 2]])
                    P.op("pe", lambda e, h=h, o_=o_: e.matmul(out=o_, lhsT=qs[:, h, :], rhs=Cst[:, h, 0:256], start=False, stop=True),
                         reads=[qsB[hg], CB[h]], writes=[bN[h4 // 2]])
                for h4 in range(4):
                    h = hg * 4 + h4
                    d_ = pK[:, 256 + h4: 256 + h4 + 1]
                    P.op("pe", lambda e, h=h, d_=d_: e.matmul(out=d_, lhsT=swT[:, h, :], rhs=oneb[:], start=True, stop=False),
                         reads=[swB[hg], cB, kkB[hg]], writes=[bK])
                    P.op("pe", lambda e, h=h, d_=d_: e.matmul(out=d_, lhsT=qs[:, h, :], rhs=Cst[:, h, 256:257], start=False, stop=True),
                         reads=[qsB[hg], CB[h]], writes=[bK])
                pC = [pC0, pC1]
                bC = [bC0, bC1]
                for h4 in range(4):
                    h = hg * 4 + h4
                    o_ = pC[h4 // 2][:, (h4 % 2) * 256:(h4 % 2) * 256 + 256]
                    P.op("pe", lambda e, h=h, o_=o_: e.matmul(out=o_, lhsT=kk[:, h, :], rhs=vw[:, h, 0:256], start=True, stop=True),
                         reads=[kkB[hg], vwB[hg]], writes=[bC[h4 // 2]])
                for h4 in range(4):
                    h = hg * 4 + h4
                    d_ = pK[:, 264 + h4: 264 + h4 + 1]
                    P.op("pe", lambda e, h=h, d_=d_: e.matmul(out=d_, lhsT=kk[:, h, :], rhs=vw[:, h, 256:257], start=True, stop=True),
                         reads=[kkB[hg], vwB[hg]], writes=[bK])
                deng = den[:, hs]
                P.op("act", lambda e, deng=deng: e.activation(out=deng, in_=pK[:, 256:260], func=AF.Abs), reads=[bK], writes=[smB[hg]])
                P.op("dve", lambda e, deng=deng: e.tensor_scalar(out=deng, in0=deng, scalar1=1.0, scalar2=None, op0=ALU.max), reads=[], writes=[smB[hg]])
                P.op("dve", lambda e, deng=deng: e.reciprocal(out=deng, in_=deng), reads=[], writes=[smB[hg]])
                for h4 in range(4):
                    h = hg * 4 + h4
                    o_ = pN[h4 // 2][:, (h4 % 2) * 256:(h4 % 2) * 256 + 256]
                    P.op("act", lambda e, h=h, o_=o_: e.activation(out=hh[:, h, :], in_=o_, func=AF.Identity, scale=den[:, h:h + 1]),
                         reads=[bN[h4 // 2], smB[hg]], writes=[hhB[hg]])
                for h4 in range(4):
                    h = hg * 4 + h4
                    o_ = pC[h4 // 2][:, (h4 % 2) * 256:(h4 % 2) * 256 + 256]
                    P.op("dve", lambda e, h=h, o_=o_: e.scalar_tensor_tensor(out=Cst[:, h, 0:256], in0=Cst[:, h, 0:256], scalar=EBs[:, h, 127:128], in1=o_,
                                                                            op0=ALU.mult, op1=ALU.add), reads=[bC[h4 // 2], EBB[hg]], writes=[CB[h]])
                    P.op("dve", lambda e, h=h, h4=h4: e.scalar_tensor_tensor(out=Cst[:, h, 256:257], in0=Cst[:, h, 256:257], scalar=EBs[:, h, 127:128],
                                                                            in1=pK[:, 264 + h4:265 + h4], op0=ALU.mult, op1=ALU.add),
                         reads=[bK, EBB[hg]], writes=[CB[h]])
                for h4 in range(4):
                    h = hg * 4 + h4
                    P.op("dve", lambda e, h=h: e.bn_stats(out=bst[:, h, :], in_=hh[:, h, :]), reads=[hhB[hg]], writes=[smB[hg]])
                    P.op("dve", lambda e, h=h: e.bn_aggr(out=mv[:, h, :], in_=bst[:, h, :]), reads=[], writes=[smB[hg]])
                rg = rstd[:, hs]
                P.op("act", lambda e, rg=rg, hs=hs: e.activation(out=rg, in_=mv[:, hs, 1], func=AF.Ln, bias=C.eps_ln[:], scale=1.0), reads=[C.constB], writes=[smB[hg]])
                P.op("act", lambda e, rg=rg: e.activation(out=rg, in_=rg, func=AF.Exp, scale=-0.5), reads=[], writes=[smB[hg]])
                nbg = nbias[:, hs]
                P.op("dve", lambda e, nbg=nbg, rg=rg, hs=hs: e.scalar_tensor_tensor(out=nbg, in0=mv[:, hs, 0], scalar=-1.0, in1=rg, op0=ALU.mult, op1=ALU.mult),
                     reads=[], writes=[smB[hg]])
                for h4 in range(4):
                    h = hg * 4 + h4
                    P.op("act", lambda e, h=h: e.activation(out=hh[:, h, :], in_=hh[:, h, :], func=AF.Identity, bias=nbias[:, h:h + 1], scale=rstd[:, h:h + 1]),
                         reads=[smB[hg]], writes=[hhB[hg]])
                hflat = hh[:, hg * 4:hg * 4 + 4, :].rearrange("p a b -> p (a b)")
                P.op("pool", lambda e, hflat=hflat, hg=hg: e.tensor_tensor(out=hflat, in0=hflat, in1=nwb[:, hg * 1024:(hg + 1) * 1024], op=ALU.mult),
                     reads=[nwB], writes=[hhB[hg]])
                P.op("dve", lambda e, i2=i2, hg=hg, hflat=hflat: e.tensor_tensor(out=hgt[i2][:, hg * 1024:(hg + 1) * 1024], in0=hflat,
                                                                                in1=sot[i2][:, hg * 1024:(hg + 1) * 1024], op=ALU.mult),
                     reads=[hhB[hg], sotB[i2]], writes=[hgtB[i2]])

            prefetch(0)
            prefetch(1)
            emit_R(0)
            pre(0, 0)
            pre(0, 1)
            for tt in range(16):
                c0 = tt * 128
                i2 = tt % 2
                if tt + 1 < 16:
                    emit_R(tt + 1)
                    pre(tt + 1, 0)
                    pre(tt + 1, 1)
                post(tt, 0)
                post(tt, 1)
                P.dma("sp", lambda e, i2=i2, c0=c0: e.dma_start(out=hg_d[c0:c0 + 128, :], in_=hgt[i2][:]), reads=[hgtB[i2]], writes=[hgB], par=True)
                if tt + 2 < 16:
                    prefetch(tt + 2)
            P.barrier()


def phase_out(C, a_d, aB, fm, W, x_res, xresB, g, b, x_out, xoutB):
    P = C.P
    with contextlib.ExitStack() as es:
        xs2 = [C.sb([128, 4, D], F32, es=es) for _ in range(2)]
        xsb2 = [[Buf() for _ in range(4)] for _ in range(2)]
        aT2 = [C.sb([128, 16, 512], BF16, es=es) for _ in range(2)]
        aTb2 = [Buf() for _ in range(2)]
        at = C.sb([128, 4, D], BF16, es=es) if not fm else None
        atB = [Buf() for _ in range(4)]
        identb = C.sb([128, 128], BF16, es=es)
        idB = Buf()
        P.op("dve", lambda e: e.tensor_copy(out=identb[:], in_=C.ident[:]), reads=[C.identB], writes=[idB])
        wres = C.sb([128, 16, D], BF16, es=es)
        wresB = Buf()
        for nt in range(4):
            P.dma("pool", lambda e, nt=nt: e.dma_start(out=wres[:, :, nt * 512:(nt + 1) * 512],
                                                       in_=W[:, nt * 512:(nt + 1) * 512].rearrange("(k p) n -> p k n", p=128)), writes=[wresB], par=True)
        gb = C.sb([128, D], F32, es=es)
        bb = C.sb([128, D], F32, es=es)
        gbB = Buf()
        bcast_load(C, gb, g, gbB)
        bcast_load(C, bb, b, gbB)
        st2 = ln_act_sets(C, es, 2)
        pb = [C.ps([128, 512], F32, es=es) for _ in range(6)]
        pbB = [Buf() for _ in range(6)]
        def prep(tt):
            row0 = tt * 512
            xs, xsb, aT, aTb = xs2[tt % 2], xsb2[tt % 2], aT2[tt % 2], aTb2[tt % 2]
            for sub in range(4):
                P.dma("sp", lambda e, sub=sub, row0=row0, xs=xs: e.dma_start(out=xs[:, sub, :], in_=x_res[row0 + sub * 128: row0 + (sub + 1) * 128, :]),
                      reads=[xresB], writes=[xsb[sub]])
            if fm:
                P.dma("sp", lambda e, row0=row0, aT=aT: e.dma_start(out=aT[:], in_=a_d[:, row0:row0 + 512].rearrange("(k p) n -> p k n", p=128)),
                      reads=[aB], writes=[aTb])
            else:
                for sub in range(4):
                    P.dma("sp", lambda e, sub=sub, row0=row0: e.dma_start(out=at[:, sub, :], in_=a_d[row0 + sub * 128: row0 + (sub + 1) * 128, :]),
                          reads=[aB], writes=[atB[sub]])
                k = 0
                for sub in range(4):
                    for g8 in range(2):
                        pt = pb[4 + k % 2][:].bitcast(BF16).rearrange("p (a b) -> p a b", a=8)
                        ptb = pbB[4 + k % 2]
                        for j in range(8):
                            kc = g8 * 8 + j
                            P.op("pe", lambda e, pt=pt, j=j, kc=kc, sub=sub: e.transpose(out=pt[:, j, :], in_=at[:, sub, kc * 128:(kc + 1) * 128], identity=identb[:]),
                                 reads=[atB[sub], idB], writes=[ptb])
                        dst = aT[:, g8 * 8:(g8 + 1) * 8, sub * 128:(sub + 1) * 128]
                        if k % 2 == 0:
                            P.op("act", lambda e, pt=pt, dst=dst: e.activation(out=dst, in_=pt, func=AF.Copy), reads=[ptb], writes=[aTb])
                        else:
                            P.op("dve", lambda e, pt=pt, dst=dst: e.tensor_copy(out=dst, in_=pt), reads=[ptb], writes=[aTb])
                        k += 1
        prep(0)
        for tt in range(S // 512):
            row0 = tt * 512
            xs, xsb, aT, aTb = xs2[tt % 2], xsb2[tt % 2], aT2[tt % 2], aTb2[tt % 2]
            if tt + 1 < S // 512:
                prep(tt + 1)
            stage_tm(C, None, lambda kc, sub, aT=aT: aT[:, kc, sub * 128:(sub + 1) * 128], lambda kc, sub, aTb=aTb: [aTb], 16, W, D, xs, xsb, pb[0:4], pbB[0:4],
                     wres=wres, wresB=wresB)
            def store(sub, row0=row0, xs=xs, xsb=xsb):
                P.dma("sp", lambda e, sub=sub, row0=row0, xs=xs: e.dma_start(out=x_out[row0 + sub * 128: row0 + (sub + 1) * 128, :], in_=xs[:, sub, :]),
                      reads=[xsb[sub]], writes=[xoutB], par=True)
            layernorm_tiles_act(C, [xs[:, sub, :] for sub in range(4)], xsb, gb, bb, gbB, st2, store)
        P.barrier()


def host_consts():
    j = np.arange(128)[:, None]
    t = np.arange(128)[None, :]
    return {
        "ident": np.eye(128, dtype=np.float32),
        "U": (j > t).astype(np.float32),
        "LT": (j <= t).astype(np.float32),
        "MB": np.where(j > t, -30000.0, 0.0).astype(np.float32),
    }


def rms_norm_fm(C, cT, cTb, cnT, cnTb, gn, gnB, ONE, oneB, sqst, sqB, rs, rsB, bank, bankB):
    P = C.P
    for tokt in range(4):
        ts_ = slice(tokt * 512, (tokt + 1) * 512)
        for fc in range(4):
            i = fc % 2
            P.op("pool", lambda e, i=i, fc=fc, ts_=ts_: e.tensor_tensor(out=sqst[i][:], in0=cT[:, fc, ts_], in1=cT[:, fc, ts_], op=ALU.mult),
                 reads=[cTb], writes=[sqB[i]])
            P.op("pe", lambda e, i=i, fc=fc: e.matmul(out=bank[:], lhsT=ONE[:], rhs=sqst[i][:], start=(fc == 0), stop=(fc == 3)),
                 reads=[sqB[i], oneB], writes=[bankB])
        P.op("act", lambda e: e.activation(out=rs[:], in_=bank[:], func=AF.Ln, bias=C.eps_rms[:], scale=1.0 / 512), reads=[bankB, C.constB], writes=[rsB])
        P.op("act", lambda e: e.activation(out=rs[:], in_=rs[:], func=AF.Exp, scale=-0.5), reads=[], writes=[rsB])
        for fc in range(4):
            P.op("dve", lambda e, fc=fc, ts_=ts_: e.scalar_tensor_tensor(out=cnT[:, fc, ts_], in0=cT[:, fc, ts_], scalar=gn[:, fc:fc + 1], in1=rs[:],
                                                                        op0=ALU.mult, op1=ALU.mult), reads=[cTb, gnB, rsB], writes=[cnTb])


def phase_kvq(C, x_in, xinB, w_down, kv_norm_w, w_up, w_dq, q_norm_w, w_uq, consts, knT_d, krT_d, v_d, qnT_d, qrT_d, kvB):
    P = C.P
    with contextlib.ExitStack() as es:
        xT = C.sb([128, 16, S], BF16, es=es)
        xTb = Buf()
        pb = [C.ps([128, 512], F32, es=es) for _ in range(8)]
        pbB = [Buf() for _ in range(8)]
        with contextlib.ExitStack() as es1:
            xs = C.sb([128, 4, D], F32, es=es1)
            xsb = [Buf() for _ in range(4)]
            build_xT_full(C, x_in, xinB, xT, xTb, xs, xsb, pb[4:8], pbB[4:8])
            P.barrier()
        ring = PanelRing(C, 2, es)
        cT = C.sb([128, 4, S], F32, es=es)
        cTb = Buf()
        cnT = C.sb([128, 4, S], BF16, es=es)
        cnTb = Buf()
        cos2 = C.sb([128, S], F32, es=es)
        sinS = C.sb([128, S], F32, es=es)
        tabB = Buf()
        P.dma("sp", lambda e: e.dma_start(out=cos2[:], in_=consts["cos2"]), writes=[tabB], par=True)
        P.dma("sp", lambda e: e.dma_start(out=sinS[:], in_=consts["sinS"]), writes=[tabB], par=True)
        gn = C.sb([128, 8], F32, es=es)
        gnB = Buf()
        P.dma("sp", lambda e: e.dma_start(out=gn[:, 0:4], in_=kv_norm_w.rearrange("(c p) -> p c", p=128), allow_slow_non_contiguous=True), writes=[gnB], par=True)
        P.dma("sp", lambda e: e.dma_start(out=gn[:, 4:8], in_=q_norm_w.rearrange("(c p) -> p c", p=128), allow_slow_non_contiguous=True), writes=[gnB], par=True)
        ONE = C.sb([128, 128], F32, es=es)
        oneB = Buf()
        P.op("dve", lambda e: e.memset(ONE[:], 1.0), writes=[oneB])
        sqst = [C.sb([128, 512], F32, es=es) for _ in range(2)]
        sqB = [Buf() for _ in range(2)]
        rs = C.sb([128, 512], F32, es=es)
        rsB = Buf()
        kr1 = C.sb([128, S], F32, es=es)
        kr1B = Buf()
        krb = C.sb([128, S], BF16, es=es)
        krbB = Buf()
        tmp = [C.sb([128, 512], F32, es=es) for _ in range(2)]
        tmpB = [Buf() for _ in range(2)]
        ost = [C.sb([128, 512], BF16, es=es) for _ in range(4)]
        ostB = [Buf() for _ in range(4)]
        cnt = [0]
        plan = [panel_src(w_down, 0, 0),
                cols_src(w_down, 0, 16, [(512, 64), (512, 64)]),
                cols_src(w_down, 0, 16, [(544, 32), (512, 32), (544, 32), (512, 32)])]
        for pg in range(4):
            plan.append(cols_src(w_up, 0, 4, [(h * 256, 128) for h in range(pg * 4, pg * 4 + 4)]))
        for pg in range(4):
            plan.append(cols_src(w_up, 0, 4, [(h * 256 + 128, 128) for h in range(pg * 4, pg * 4 + 4)]))
        plan.append(panel_src(w_dq, 0, 0))
        for pg in range(4):
            plan.append(cols_src(w_uq, 0, 4, [(h * 192, 128) for h in range(pg * 4, pg * 4 + 4)]))
        for pr in range(8):
            h0, h1 = 2 * pr, 2 * pr + 1
            plan.append(cols_src(w_uq, 0, 4, [(h0 * 192 + 128, 64), (h1 * 192 + 128, 64)]))
            plan.append(cols_src(w_uq, 0, 4, [(h0 * 192 + 160, 32), (h0 * 192 + 128, 32), (h1 * 192 + 160, 32), (h1 * 192 + 128, 32)]))
        ring.plan(plan)

        def store(bank, bankB, dst_fn, m=128, eng_i=[0]):
            i = cnt[0] % 4
            cnt[0] += 1
            if i % 2 == 0:
                P.op("act", lambda e, i=i, bank=bank: e.activation(out=ost[i][:], in_=bank[:], func=AF.Copy), reads=[bankB], writes=[ostB[i]])
            else:
                P.op("dve", lambda e, i=i, bank=bank: e.tensor_copy(out=ost[i][:], in_=bank[:]), reads=[bankB], writes=[ostB[i]])
            P.dma("sp", lambda e, i=i: e.dma_start(out=dst_fn(), in_=ost[i][:]), reads=[ostB[i]], writes=[kvB], par=True)

        def ev_c(fc, tokt, bank, bankB, m):
            P.op("act", lambda e, fc=fc, tokt=tokt, bank=bank: e.activation(out=cT[:, fc, tokt * 512:(tokt + 1) * 512], in_=bank[:], func=AF.Copy),
                 reads=[bankB], writes=[cTb])
        lin_fm(C, ring, xT, xTb, 16, 512, pb[0:4], pbB[0:4], ev_c)

        def rope_pair(src, srcb, KC, dst_fn):
            def ev1(fc, tokt, bank, bankB, m):
                P.op("dve", lambda e, tokt=tokt, bank=bank: e.tensor_tensor(out=kr1[:, tokt * 512:(tokt + 1) * 512], in0=bank[:], in1=cos2[:, tokt * 512:(tokt + 1) * 512],
                                                                           op=ALU.mult), reads=[bankB, tabB], writes=[kr1B])
            lin_fm(C, ring, src, srcb, KC, 128, pb[0:4], pbB[0:4], ev1)

            def ev2(fc, tokt, bank, bankB, m):
                i = tokt % 2
                P.op("dve", lambda e, i=i, tokt=tokt, bank=bank: e.tensor_tensor(out=tmp[i][:], in0=bank[:], in1=sinS[:, tokt * 512:(tokt + 1) * 512], op=ALU.mult),
                     reads=[bankB, tabB], writes=[tmpB[i]])
                dst_fn(i, tokt)
            lin_fm(C, ring, src, srcb, KC, 128, pb[0:4], pbB[0:4], ev2)

        def kr_dst(i, tokt):
            P.op("pool", lambda e, i=i, tokt=tokt: e.tensor_tensor(out=krb[:, tokt * 512:(tokt + 1) * 512], in0=kr1[:, tokt * 512:(tokt + 1) * 512], in1=tmp[i][:], op=ALU.add),
                 reads=[kr1B, tmpB[i]], writes=[krbB])
        rope_pair(xT, xTb, 16, kr_dst)
        P.dma("sp", lambda e: e.dma_start(out=krT_d, in_=krb[:]), reads=[krbB], writes=[kvB], par=True)

        rms_norm_fm(C, cT, cTb, cnT, cnTb, gn[:, 0:4], gnB, ONE, oneB, sqst, sqB, rs, rsB, pb[4], pbB[4])
        for pg in range(4):
            def ev_kn(fc, tokt, bank, bankB, m, pg=pg):
                h = pg * 4 + fc
                store(bank, bankB, lambda h=h, tokt=tokt: knT_d[h, :, tokt * 512:(tokt + 1) * 512])
            lin_fm(C, ring, cnT, cnTb, 4, 512, pb[0:4], pbB[0:4], ev_kn)
        for pg in range(4):
            def ev_v(tokt, bank, bankB, pg=pg):
                store(bank, bankB, lambda pg=pg, tokt=tokt: v_d[pg * 4:(pg + 1) * 4, tokt * 128:(tokt + 1) * 128, :].rearrange("h t v -> t h v"))
            lin_tm(C, ring, cnT, cnTb, 4, 512, pb[0:4], pbB[0:4], ev_v)

        lin_fm(C, ring, xT, xTb, 16, 512, pb[0:4], pbB[0:4], ev_c)
        rms_norm_fm(C, cT, cTb, cnT, cnTb, gn[:, 4:8], gnB, ONE, oneB, sqst, sqB, rs, rsB, pb[4], pbB[4])
        for pg in range(4):
            def ev_qn(fc, tokt, bank, bankB, m, pg=pg):
                h = pg * 4 + fc
                store(bank, bankB, lambda h=h, tokt=tokt: qnT_d[h, :, tokt * 512:(tokt + 1) * 512])
            lin_fm(C, ring, cnT, cnTb, 4, 512, pb[0:4], pbB[0:4], ev_qn)
        for pr in range(8):
            def q_dst(i, tokt, pr=pr):
                j = cnt[0] % 4
                cnt[0] += 1
                P.op("pool", lambda e, i=i, j=j, tokt=tokt: e.tensor_tensor(out=ost[j][:], in0=kr1[:, tokt * 512:(tokt + 1) * 512], in1=tmp[i][:], op=ALU.add),
                     reads=[kr1B, tmpB[i]], writes=[ostB[j]])
                P.dma("sp", lambda e, j=j, pr=pr, tokt=tokt: e.dma_start(out=qrT_d[pr, :, tokt * 512:(tokt + 1) * 512], in_=ost[j][:]),
                      reads=[ostB[j]], writes=[kvB], par=True)
            rope_pair(cnT, cnTb, 4, q_dst)
        P.barrier()


def phase_att(C, knT_d, krT_d, v_d, qnT_d, qrT_d, kvB, oT_d, oTB):
    P = C.P
    SCALE = 192 ** -0.5
    with contextlib.ExitStack() as es:
        kr = C.sb([128, S], BF16, es=es)
        krB = Buf()
        P.dma("sp", lambda e: e.dma_start(out=kr[:], in_=krT_d), reads=[kvB], writes=[krB])
        kn = [C.sb([128, S], BF16, es=es) for _ in range(2)]
        qn = [C.sb([128, S], BF16, es=es) for _ in range(2)]
        qr = [C.sb([128, S], BF16, es=es) for _ in range(2)]
        vv = [C.sb([128, 16, 128], BF16, es=es) for _ in range(2)]
        hB = [Buf() for _ in range(2)]
        onesb = C.sb([128, 128], BF16, es=es)
        oB = Buf()
        P.op("dve", lambda e: e.memset(onesb[:], 1.0), writes=[oB])
        pT = [C.sb([128, 512], BF16, es=es) for _ in range(4)]
        pTB = [Buf() for _ in range(4)]
        rsum = C.sb([128, 512], F32, es=es)
        rsB = Buf()
        ot = [C.sb([128, 512], BF16, es=es) for _ in range(2)]
        otB = [Buf() for _ in range(2)]
        pS = [C.ps([128, 512], F32, es=es) for _ in range(3)]
        pSB = [Buf() for _ in range(3)]
        pO = [C.ps([128, 512], F32, es=es) for _ in range(2)]
        pOB = [Buf() for _ in range(2)]
        pM = [C.ps([128, 512], F32, es=es) for _ in range(2)]
        pMB = [Buf() for _ in range(2)]

        def load_head(h):
            i = h % 2
            P.dma("sp", lambda e, i=i, h=h: e.dma_start(out=kn[i][:], in_=knT_d[h]), reads=[kvB], writes=[hB[i]])
            P.dma("sp", lambda e, i=i, h=h: e.dma_start(out=qn[i][:], in_=qnT_d[h]), reads=[kvB], writes=[hB[i]], par=True)
            P.dma("sp", lambda e, i=i, h=h: e.dma_start(out=qr[i][:], in_=qrT_d[h // 2]), reads=[kvB], writes=[hB[i]], par=True)
            P.dma("sp", lambda e, i=i, h=h: e.dma_start(out=vv[i][:], in_=v_d[h].rearrange("(t p) v -> p t v", p=128)), reads=[kvB], writes=[hB[i]], par=True)
        load_head(0)
        blk = 0
        oc = 0
        pend = None

        def flush():
            nonlocal pend
            if pend is not None:
                pend()
                pend = None
        for h in range(16):
            i = h % 2
            flush()
            if h + 1 < 16:
                load_head(h + 1)
            r0 = (h % 2) * 64
            for qt in range(4):
                po, poB = pO[oc % 2], pOB[oc % 2]
                pm, pmB = pM[oc % 2], pMB[oc % 2]
                nk = 4 * qt + 4
                for kt in range(nk):
                    j = kt - 4 * qt
                    c0 = max(0, j) * 128
                    n = 512 - c0
                    ps_, psB = pS[blk % 3], pSB[blk % 3]
                    pt, ptB = pT[blk % 4], pTB[blk % 4]
                    blk += 1
                    q0 = qt * 512 + c0
                    P.op("pe", lambda e, i=i, kt=kt, q0=q0, n=n, ps_=ps_: e.matmul(out=ps_[:, 0:n], lhsT=kn[i][:, kt * 128:(kt + 1) * 128], rhs=qn[i][:, q0:q0 + n],
                                                                                  start=True, stop=False), reads=[hB[i]], writes=[psB])
                    P.op("pe", lambda e, i=i, kt=kt, q0=q0, n=n, ps_=ps_, r0=r0: e.matmul(out=ps_[:, 0:n], lhsT=kr[r0:r0 + 64, kt * 128:(kt + 1) * 128],
                                                                                         rhs=qr[i][r0:r0 + 64, q0:q0 + n], start=False, stop=True),
                         reads=[hB[i], krB], writes=[psB])
                    P.op("act", lambda e, pt=pt, ps_=ps_, n=n: e.activation(out=pt[:, 0:n], in_=ps_[:, 0:n], func=AF.Exp, scale=SCALE), reads=[psB], writes=[ptB])
                    if j >= 0:
                        P.op("pool", lambda e, pt=pt: e.memset(pt[64:128, 0:64], 0.0), reads=[], writes=[ptB])
                    flush()

                    def pv(i=i, kt=kt, pt=pt, ptB=ptB, po=po, poB=poB, pm=pm, pmB=pmB, c0=c0, n=n, nk=nk):
                        P.op("pe", lambda e: e.matmul(out=po[:, c0:c0 + n], lhsT=vv[i][:, kt, :], rhs=pt[:, 0:n], start=(kt == 0), stop=(kt == nk - 1)),
                             reads=[hB[i], ptB], writes=[poB])
                        P.op("pe", lambda e: e.matmul(out=pm[:, c0:c0 + n], lhsT=onesb[:], rhs=pt[:, 0:n], start=(kt == 0), stop=(kt == nk - 1)),
                             reads=[oB, ptB], writes=[pmB])
                    pend = pv
                    if kt == nk - 1:
                        def fin(po=po, poB=poB, pm=pm, pmB=pmB, h=h, qt=qt, oc=oc, pv=pv):
                            pv()
                            P.op("dve", lambda e: e.reciprocal(out=rsum[:], in_=pm[:]), reads=[pmB], writes=[rsB])
                            o_, o_B = ot[oc % 2], otB[oc % 2]
                            P.op("dve", lambda e: e.tensor_tensor(out=o_[:], in0=po[:], in1=rsum[:], op=ALU.mult), reads=[poB, rsB], writes=[o_B])
                            P.dma("sp", lambda e: e.dma_start(out=oT_d[h * 128:(h + 1) * 128, qt * 512:(qt + 1) * 512], in_=o_[:]),
                                  reads=[o_B], writes=[oTB], par=True)
                        pend = fin
                oc += 1
        flush()
        P.barrier()


def rope_tables():
    inv_freq = (np.float32(10000.0) ** (-np.arange(0, 64, 2, dtype=np.float32) / np.float32(64))).astype(np.float32)
    ang = np.arange(S, dtype=np.float32)[:, None] * inv_freq[None, :]
    cos = np.cos(ang.astype(np.float64)).astype(np.float32).T
    sin = np.sin(ang.astype(np.float64)).astype(np.float32).T
    cos2 = np.concatenate([cos, cos, cos, cos], axis=0)
    sinS = np.concatenate([-sin, sin, -sin, sin], axis=0)
    return {"cos2": np.ascontiguousarray(cos2), "sinS": np.ascontiguousarray(sinS)}


W_SHAPES = {
    "a_w_in": [D, 6160], "a_b_gates": [16], "a_conv_w": [4, 2048], "a_conv_b": [2048], "a_norm_w": [2048], "a_w_out": [2048, 2048],
    "kv_w_down": [D, 576], "kv_norm_w": [512], "kv_w_up": [512, 4096], "b_w_dq": [D, 512], "b_q_norm_w": [512], "b_w_uq": [512, 3072],
    "b_w_out": [2048, 2048], "mlp_w1": [2, D, DFF], "mlp_w2": [2, DFF, D], "ln1_g": [2, D], "ln1_b": [2, D], "ln2_g": [2, D], "ln2_b": [2, D],
}


def all_consts():
    hc = dict(host_consts())
    hc.update(rope_tables())
    return hc


def build_full():
    nc = bass.Bass("TRN2", target_bir_lowering=False)

    def dt(n, sh, kind=None, d=F32):
        if kind is None:
            return nc.dram_tensor(n, sh, d).ap()
        return nc.dram_tensor(n, sh, d, kind=kind).ap()
    x = dt("x", [S, D], "ExternalInput")
    w = {k: dt(k, sh, "ExternalInput") for k, sh in W_SHAPES.items()}
    consts = {k: dt("c_" + k, list(v.shape), "ExternalInput") for k, v in all_consts().items()}
    out = dt("out", [S, D], "ExternalOutput")
    v_s = dt("v_s", [S, 2048], None, BF16)
    so_s = dt("so_s", [S, 2048])
    hg_d = dt("hg_d", [S, 2048], None, BF16)
    x1 = dt("x1", [S, D])
    x2 = dt("x2", [S, D])
    x3 = dt("x3", [S, D])
    knT_d = dt("knT_d", [16, 128, S], None, BF16)
    krT_d = dt("krT_d", [128, S], None, BF16)
    v_d = dt("v_d", [16, S, 128], None, BF16)
    qnT_d = dt("qnT_d", [16, 128, S], None, BF16)
    qrT_d = dt("qrT_d", [8, 128, S], None, BF16)
    oT_d = dt("oT_d", [2048, S], None, BF16)
    with contextlib.ExitStack() as es:
        C = Ctx(nc, es)
        load_consts(C, consts)
        xB, hgB, x1B, x2B, x3B, kvB, oTB, outB = [Buf() for _ in range(8)]
        phase_mlstm(C, x, xB, w["a_w_in"], w["a_b_gates"], w["a_conv_w"], w["a_conv_b"], w["a_norm_w"], consts, v_s, so_s, hg_d, hgB)
        phase_out(C, hg_d, hgB, False, w["a_w_out"], x, xB, w["ln1_g"][0], w["ln1_b"][0], x1, x1B)
        phase_mlp2(C, x1, w["mlp_w1"][0], w["mlp_w2"][0], w["ln2_g"][0], w["ln2_b"][0], x2, x1B, x2B)
        phase_kvq(C, x2, x2B, w["kv_w_down"], w["kv_norm_w"], w["kv_w_up"], w["b_w_dq"], w["b_q_norm_w"], w["b_w_uq"], consts,
                  knT_d, krT_d, v_d, qnT_d, qrT_d, kvB)
        phase_att(C, knT_d, krT_d, v_d, qnT_d, qrT_d, kvB, oT_d, oTB)
        phase_out(C, oT_d, oTB, True, w["b_w_out"], x2, x2B, w["ln1_g"][1], w["ln1_b"][1], x3, x3B)
        phase_mlp2(C, x3, w["mlp_w1"][1], w["mlp_w2"][1], w["ln2_g"][1], w["ln2_b"][1], out, x3B, outB)
        block = es.enter_context(nc.Block())
        C.P.emit(block, es)
    return nc


def kernel(**inputs):
    x = np.ascontiguousarray(np.asarray(inputs["x"], dtype=np.float32))
    consts = all_consts()
    shared = {}
    for k, sh in W_SHAPES.items():
        a = np.asarray(inputs[k], dtype=np.float32)
        shared[k] = np.ascontiguousarray(a.reshape(sh))
    for k, v in consts.items():
        shared["c_" + k] = v
    nc = build_full()
    in_maps = []
    for c in range(NCORES):
        m = dict(shared)
        m["x"] = x[c]
        in_maps.append(m)
    res = run_bass_kernel_spmd(nc, in_maps, core_ids=list(range(NCORES)))
    return np.stack([np.asarray(r["out"], dtype=np.float32) for r in res.results], axis=0)


def phase_mlp2(C, x_in, w1, w2, g, b, x_out, xinB, xoutB):
    P = C.P
    NS = 8
    with contextlib.ExitStack() as es:
        xs = C.sb([128, NS, D], F32, es=es)
        xsb = [Buf() for _ in range(NS)]
        xT = C.sb([128, 16, 1024], BF16, es=es)
        xTb = Buf()
        hT = C.sb([128, 16, 1024], BF16, es=es)
        hTb = [Buf() for _ in range(16)]
        ring = PanelRing(C, 2, es)
        xstg = C.sb([128, 2, D], F32, es=es)
        xstgB = [Buf() for _ in range(2)]
        rst = [C.sb([128, 512], F32, es=es) for _ in range(2)]
        rstB = [Buf() for _ in range(2)]
        gb = C.sb([128, D], F32, es=es)
        bb = C.sb([128, D], F32, es=es)
        gbB = Buf()
        bcast_load(C, gb, g, gbB)
        bcast_load(C, bb, b, gbB)
        st = ln_stat_sets(C, es)
        pb = [C.ps([128, 512], F32, es=es) for _ in range(8)]
        pbB = [Buf() for _ in range(8)]
        for stile in range(S // 1024):
            for q in range(4):
                ring.plan([panel_src(w1, 0, q * 2048 + fg * 512) for fg in range(4)])
                ring.plan([panel_src(w2, q * 2048, nt * 512) for nt in range(4)])
        k1 = 0
        k2 = 0
        kk_ = [0]

        def build_xT(stile):
            row0 = stile * 1024
            for sub in range(NS):
                sg, sgB = xstg[:, sub % 2, :], xstgB[sub % 2]
                P.dma("sp", lambda e, sub=sub, row0=row0, sg=sg: e.dma_start(out=sg, in_=x_in[row0 + sub * 128: row0 + (sub + 1) * 128, :]),
                      reads=[xinB], writes=[sgB])
                for gq in range(4):
                    k = kk_[0]
                    kk_[0] += 1
                    pt = pb[4 + k % 4][:].rearrange("p (a b) -> p a b", a=4)
                    ptb = pbB[4 + k % 4]
                    for j in range(4):
                        kc = gq * 4 + j
                        P.op("pe", lambda e, pt=pt, j=j, kc=kc, sg=sg: e.transpose(out=pt[:, j, :], in_=sg[:, kc * 128:(kc + 1) * 128], identity=C.ident[:]),
                             reads=[sgB, C.identB], writes=[ptb])
                    dst = xT[:, gq * 4:(gq + 1) * 4, sub * 128:(sub + 1) * 128]
                    if k % 2 == 0:
                        P.op("act", lambda e, pt=pt, dst=dst: e.activation(out=dst, in_=pt, func=AF.Copy), reads=[ptb], writes=[xTb])
                    else:
                        P.op("dve", lambda e, pt=pt, dst=dst: e.tensor_copy(out=dst, in_=pt), reads=[ptb], writes=[xTb])
        build_xT(0)
        for stile in range(S // 1024):
            row0 = stile * 1024
            for sub in range(NS):
                P.dma("sp", lambda e, sub=sub, row0=row0: e.dma_start(out=xs[:, sub, :], in_=x_in[row0 + sub * 128: row0 + (sub + 1) * 128, :]),
                      reads=[xinB], writes=[xsb[sub]])
            for q in range(4):
                for fg in range(4):
                    wt, wb = ring.next()
                    for fc in range(4):
                        f = fg * 4 + fc
                        for tk in range(2):
                            bank, bankB = pb[4 + k1 % 4], pbB[4 + k1 % 4]
                            for kc in range(16):
                                P.op("pe", lambda e, wt=wt, kc=kc, fc=fc, bank=bank, tk=tk:
                                     e.matmul(out=bank[:], lhsT=wt[:, kc * 512 + fc * 128: kc * 512 + (fc + 1) * 128], rhs=xT[:, kc, tk * 512:(tk + 1) * 512],
                                              start=(kc == 0), stop=(kc == 15)), reads=[wb, xTb], writes=[bankB])
                            r, rB = rst[k1 % 2], rstB[k1 % 2]
                            P.op("act", lambda e, r=r, bank=bank: e.activation(out=r[:], in_=bank[:], func=AF.Relu), reads=[bankB], writes=[rB])
                            P.op("act", lambda e, r=r, f=f, tk=tk: e.activation(out=hT[:, f, tk * 512:(tk + 1) * 512], in_=r[:], func=AF.Square),
                                 reads=[rB], writes=[hTb[f]])
                            k1 += 1
                if q == 3 and stile + 1 < S // 1024:
                    build_xT(stile + 1)
                for nt in range(4):
                    wt, wb = ring.next()
                    for sub in range(NS):
                        bank, bankB = pb[k2 % 4], pbB[k2 % 4]
                        k2 += 1
                        for fc in range(16):
                            P.op("pe", lambda e, wt=wt, fc=fc, sub=sub, bank=bank:
                                 e.matmul(out=bank[:], lhsT=hT[:, fc, sub * 128:(sub + 1) * 128], rhs=wt[:, fc * 512:(fc + 1) * 512], start=(fc == 0), stop=(fc == 15)),
                                 reads=[wb, hTb[fc]], writes=[bankB])
                        zs = xs[:, sub, nt * 512:(nt + 1) * 512]
                        if q == 0:
                            P.op("dve", lambda e, zs=zs, bank=bank: e.scalar_tensor_tensor(out=zs, in0=zs, scalar=DN_ALPHA, in1=bank[:], op0=ALU.mult, op1=ALU.add),
                                 reads=[bankB], writes=[xsb[sub]])
                        else:
                            P.op("dve", lambda e, zs=zs, bank=bank: e.tensor_tensor(out=zs, in0=bank[:], in1=zs, op=ALU.add), reads=[bankB], writes=[xsb[sub]])
            def store(sub, row0=row0):
                P.dma("sp", lambda e, sub=sub, row0=row0: e.dma_start(out=x_out[row0 + sub * 128: row0 + (sub + 1) * 128, :], in_=xs[:, sub, :]),
                      reads=[xsb[sub]], writes=[xoutB], par=True)
            layernorm_tiles(C, [xs[:, sub, :] for sub in range(NS)], xsb, gb, bb, gbB, st, store)
        P.barrier()
```

```python
import contextlib
import numpy as np
import concourse.bass as bass
import concourse.mybir as mybir
from concourse.bass_utils import run_bass_kernel_spmd

F32 = mybir.dt.float32
BF16 = mybir.dt.bfloat16
AF = mybir.ActivationFunctionType
ALU = mybir.AluOpType
AX = mybir.AxisListType

S = 2048
D = 2048
DFF = 8192
NCORES = 8
DN_ALPHA = 4.0 ** 0.25
LN_EPS = 1e-5
RMS_EPS = 1e-6
SEG = 12000
NDMA = 20
NDMA_SP = 14


class Tok:
    __slots__ = ("eng", "idx", "dsem", "dval")

    def __init__(self, eng, idx, dsem=None, dval=None):
        self.eng, self.idx, self.dsem, self.dval = eng, idx, dsem, dval


class Buf:
    __slots__ = ("w", "r")

    def __init__(self):
        self.w = []
        self.r = []


class Prog:
    ENG = ("pe", "act", "dve", "pool", "sp")

    def __init__(self, nc):
        self.nc = nc
        self.ops = {e: [] for e in self.ENG}
        self.dma_cnt = [0] * NDMA
        self.dma_last = [None] * NDMA
        self.dma_rr = 0
        self.dma_rr_pool = 0

    def _deps(self, reads, writes, extra, par=False):
        deps = []
        for b in reads:
            deps.extend(b.w)
        for b in writes:
            if par:
                deps.extend(t for t in b.w if t.dsem is None)
            else:
                deps.extend(b.w)
            deps.extend(b.r)
        for t in extra:
            if t is not None:
                deps.append(t)
        return deps

    def _update(self, tok, reads, writes, inorder, par=False):
        for b in reads:
            if inorder:
                b.r = [t for t in b.r if not (t.dsem is None and t.eng == tok.eng)]
            b.r.append(tok)
        for b in writes:
            if par:
                b.w = [t for t in b.w if t.dsem is not None] + [tok]
            else:
                b.w = [tok]
            b.r = []

    def op(self, eng, fn, reads=(), writes=(), extra=()):
        deps = self._deps(reads, writes, extra)
        idx = len(self.ops[eng])
        self.ops[eng].append({"fn": fn, "deps": deps, "marked": False, "dma": None})
        tok = Tok(eng, idx)
        self._update(tok, reads, writes, True)
        return tok

    def dma(self, eng, fn, reads=(), writes=(), extra=(), par=False):
        deps = self._deps(reads, writes, extra, par)
        if eng == "pool":
            i = NDMA_SP + self.dma_rr_pool
            self.dma_rr_pool = (self.dma_rr_pool + 1) % (NDMA - NDMA_SP)
        else:
            i = self.dma_rr
            self.dma_rr = (self.dma_rr + 1) % NDMA_SP
        if self.dma_last[i] is not None:
            deps.append(self.dma_last[i])
        self.dma_cnt[i] += 16
        idx = len(self.ops[eng])
        tok = Tok(eng, idx, i, self.dma_cnt[i])
        self.dma_last[i] = tok
        self.ops[eng].append({"fn": fn, "deps": deps, "marked": False, "dma": tok})
        self._update(tok, reads, writes, False, par)
        return tok

    def barrier(self):
        last = []
        for e in self.ENG:
            for i in range(len(self.ops[e]) - 1, -1, -1):
                o = self.ops[e][i]
                if o["fn"] is not None and o["dma"] is None:
                    last.append(Tok(e, i))
                    break
        pend = [t for t in self.dma_last if t is not None]
        for e in self.ENG:
            deps = [t for t in last if t.eng != e] + pend
            self.ops[e].append({"fn": None, "deps": deps, "marked": False, "dma": None})

    def emit(self, block, es):
        nc = self.nc
        for e in self.ENG:
            for o in self.ops[e]:
                for t in o["deps"]:
                    if t.dsem is None and not (t.eng == "pe" and e == "pe"):
                        self.ops[t.eng][t.idx]["marked"] = True
        for e in self.ENG:
            for i, o in enumerate(self.ops[e]):
                assert not (o["marked"] and o["fn"] is None and o["dma"] is None) or True
        cnt = {}
        nmarks = {}
        for e in self.ENG:
            c = 0
            arr = []
            for o in self.ops[e]:
                if o["marked"] and o["dma"] is None and o["fn"] is not None:
                    c += 1
                arr.append(c)
            cnt[e] = arr
            nmarks[e] = c
        esems = {e: [es.enter_context(nc.semaphore("s_%s_%d" % (e, k))) for k in range(max(1, -(-nmarks[e] // SEG)))]
                 for e in self.ENG}
        dsems = [es.enter_context(nc.semaphore("d_%d" % i)) for i in range(NDMA)]
        handles = {"pe": block.tensor, "act": block.scalar, "dve": block.vector,
                   "pool": block.gpsimd, "sp": block.sync}
        final = [t for t in self.dma_last if t is not None]
        for e in self.ENG:
            ops = self.ops[e]
            plan = []
            seen_e = {}
            seen_d = {}
            ops2 = list(ops)
            if e == "sp":
                ops2 = ops2 + [{"fn": None, "deps": final, "marked": False, "dma": None}]
            for i, o in enumerate(ops2):
                waits = []
                for t in o["deps"]:
                    if t.dsem is not None:
                        if seen_d.get(t.dsem, 0) >= t.dval:
                            continue
                        seen_d[t.dsem] = t.dval
                        waits.append((dsems[t.dsem], t.dval))
                    else:
                        m = cnt[t.eng][t.idx]
                        if m == 0:
                            continue
                        if t.eng == e and e == "pe":
                            continue
                        if seen_e.get(t.eng, 0) >= m:
                            continue
                        seen_e[t.eng] = m
                        k = (m - 1) // SEG
                        waits.append((esems[t.eng][k], (m - 1) % SEG + 1))
                inc = None
                if o["dma"] is not None:
                    inc = (dsems[o["dma"].dsem], 16)
                elif o["marked"] and o["fn"] is not None:
                    m = cnt[e][i]
                    inc = (esems[e][(m - 1) // SEG], 1)
                plan.append((waits, o["fn"], inc))

            def body(eng, plan=plan):
                for (waits, fn, inc) in plan:
                    for (s, v) in waits:
                        eng.wait_ge(s, v)
                    if fn is not None:
                        inst = fn(eng)
                        if inc is not None:
                            inst.then_inc(inc[0], inc[1])
            handles[e](body)


class Ctx:
    def __init__(self, nc, es):
        self.nc, self.es = nc, es
        self.P = Prog(nc)
        self.n = 0

    def sb(self, shape, dt, es=None, name=None):
        self.n += 1
        return (es or self.es).enter_context(self.nc.sbuf_tensor(name or ("t%d" % self.n), list(shape), dt))

    def ps(self, shape, dt, es=None, name=None):
        self.n += 1
        return (es or self.es).enter_context(self.nc.psum_tensor(name or ("p%d" % self.n), list(shape), dt))


def layernorm_stats(C, z, zb, stset):
    P = C.P
    stats, mv, rstd, nb, sB = stset
    for c in range(4):
        P.op("dve", lambda e, c=c: e.bn_stats(out=stats[:, c, :], in_=z[:, c * 512:(c + 1) * 512]), reads=[zb], writes=[sB])
    P.op("dve", lambda e: e.bn_aggr(out=mv[:], in_=stats[:].rearrange("p a b -> p (a b)")), reads=[sB], writes=[sB])
    P.op("act", lambda e: e.activation(out=rstd[:], in_=mv[:, 1:2], func=AF.Ln, bias=C.eps_ln[:], scale=1.0), reads=[sB, C.constB], writes=[sB])
    P.op("act", lambda e: e.activation(out=rstd[:], in_=rstd[:], func=AF.Exp, scale=-0.5), reads=[sB], writes=[sB])
    P.op("dve", lambda e: e.scalar_tensor_tensor(out=nb[:], in0=mv[:, 0:1], scalar=-1.0, in1=rstd[:], op0=ALU.mult, op1=ALU.mult),
         reads=[sB], writes=[sB])


def layernorm_apply(C, z, zb, gb, bb, gbB, stset, mul_eng="dve"):
    P = C.P
    stats, mv, rstd, nb, sB = stset
    P.op("act", lambda e: e.activation(out=z, in_=z, func=AF.Identity, bias=nb[:], scale=rstd[:]), reads=[sB], writes=[zb])
    P.op(mul_eng, lambda e: e.tensor_tensor(out=z, in0=z, in1=gb[:], op=ALU.mult), reads=[gbB], writes=[zb])
    P.op("dve", lambda e: e.tensor_tensor(out=z, in0=z, in1=bb[:], op=ALU.add), reads=[gbB], writes=[zb])


def layernorm_tiles(C, zs, zbs, gb, bb, gbB, st, store_fn, mul_eng="dve"):
    n = len(zs)
    assert len(st) >= n
    for i in range(n):
        layernorm_stats(C, zs[i], zbs[i], st[i])
    for i in range(n):
        layernorm_apply(C, zs[i], zbs[i], gb, bb, gbB, st[i], mul_eng)
        store_fn(i)


def layernorm_tiles_act(C, zs, zbs, gb, bb, gbB, st2, store_fn):
    P = C.P
    for i in range(len(zs)):
        z, zb = zs[i], zbs[i]
        s1, s2, nm, t1, var, rstd, nb, junk, sB = st2[i % len(st2)]
        P.op("act", lambda e, z=z, s1=s1: e.activation(out=z, in_=z, func=AF.Identity, accum_out=s1[:]), reads=[], writes=[zb, sB])
        P.op("act", lambda e, z=z, s2=s2, junk=junk: e.activation(out=junk[:], in_=z, func=AF.Square, accum_out=s2[:]), reads=[zb], writes=[sB])
        P.op("pool", lambda e, s1=s1, nm=nm: e.tensor_tensor(out=nm[:], in0=s1[:], in1=C.c_negmean[:], op=ALU.mult), reads=[C.constB], writes=[sB])
        P.op("pool", lambda e, s2=s2, t1=t1: e.tensor_tensor(out=t1[:], in0=s2[:], in1=C.c_invd[:], op=ALU.mult), reads=[C.constB], writes=[sB])
        P.op("pool", lambda e, nm=nm, var=var: e.tensor_tensor(out=var[:], in0=nm[:], in1=nm[:], op=ALU.mult), reads=[], writes=[sB])
        P.op("pool", lambda e, t1=t1, var=var: e.tensor_tensor(out=var[:], in0=t1[:], in1=var[:], op=ALU.subtract), reads=[], writes=[sB])
        P.op("act", lambda e, var=var, rstd=rstd: e.activation(out=rstd[:], in_=var[:], func=AF.Ln, bias=C.eps_ln[:], scale=1.0), reads=[C.constB], writes=[sB])
        P.op("act", lambda e, rstd=rstd: e.activation(out=rstd[:], in_=rstd[:], func=AF.Exp, scale=-0.5), reads=[], writes=[sB])
        P.op("pool", lambda e, nm=nm, rstd=rstd, nb=nb: e.tensor_tensor(out=nb[:], in0=nm[:], in1=rstd[:], op=ALU.mult), reads=[], writes=[sB])
        P.op("act", lambda e, z=z, nb=nb, rstd=rstd: e.activation(out=z, in_=z, func=AF.Identity, bias=nb[:], scale=rstd[:]), reads=[sB], writes=[zb])
        P.op("pool", lambda e, z=z: e.tensor_tensor(out=z, in0=z, in1=gb[:], op=ALU.mult), reads=[gbB], writes=[zb])
        P.op("pool", lambda e, z=z: e.tensor_tensor(out=z, in0=z, in1=bb[:], op=ALU.add), reads=[gbB], writes=[zb])
        store_fn(i)


def ln_act_sets(C, es, n=4):
    out = []
    for _ in range(n):
        out.append(tuple(C.sb([128, 1], F32, es=es) for _ in range(7)) + (C.sb([128, D], BF16, es=es), Buf()))
    return out


def ln_stat_sets(C, es, n=8):
    return [(C.sb([128, 4, 6], F32, es=es), C.sb([128, 2], F32, es=es), C.sb([128, 1], F32, es=es), C.sb([128, 1], F32, es=es), Buf()) for _ in range(n)]


class PanelRing:
    def __init__(self, C, n, es):
        self.C = C
        self.n = n
        self.bufs = [C.sb([128, 16 * 512], BF16, es=es) for _ in range(n)]
        self.tb = [Buf() for _ in range(n)]
        self.srcs = []
        self.issued = 0
        self.taken = 0

    def plan(self, srcs):
        self.srcs.extend(srcs)

    def _issue(self):
        k = self.issued
        t, b = self.bufs[k % self.n], self.tb[k % self.n]
        for j, (dst, src) in enumerate(self.srcs[k](t)):
            self.C.P.dma("pool", lambda e, dst=dst, src=src: e.dma_start(out=dst, in_=src), writes=[b], par=(j > 0))
        self.issued += 1

    def next(self):
        while self.issued < len(self.srcs) and self.issued < self.taken + self.n:
            if self.issued - self.n >= self.taken:
                break
            self._issue()
        k = self.taken
        self.taken += 1
        return self.bufs[k % self.n], self.tb[k % self.n]


def panel_src(W, r0, c0, ncols=512, krows=16):
    def src(t):
        return [(t[:, 0:krows * ncols].rearrange("p (k n) -> p k n", k=krows),
                 W[r0:r0 + krows * 128, c0:c0 + ncols].rearrange("(k p) n -> p k n", p=128))]
    return src


def stage_tm_srcs(W, KC, ncols):
    return [panel_src(W, q * 2048, nt * 512) for nt in range(ncols // 512) for q in range(KC // 16)]


def stage_tm(C, ring, lhsT_fn, lhsT_bufs, KC, W, ncols, xs, xsb, pbanks, pbb, alpha_res=True, wres=None, wresB=None):
    P = C.P
    NQ = KC // 16
    for nt in range(ncols // 512):
        for q in range(NQ):
            if wres is None:
                wt, wb = ring.next()
                rhs_fn = lambda fc, wt=wt: wt[:, fc * 512:(fc + 1) * 512]
            else:
                wb = wresB[nt] if isinstance(wresB, list) else wresB
                rhs_fn = lambda fc, nt=nt: wres[:, fc, nt * 512:(nt + 1) * 512]
            for sub in range(4):
                for fc in range(16):
                    kc = q * 16 + fc
                    P.op("pe", lambda e, rhs_fn=rhs_fn, fc=fc, kc=kc, sub=sub, first=(kc == 0), last=(kc == KC - 1):
                         e.matmul(out=pbanks[sub][:], lhsT=lhsT_fn(kc, sub), rhs=rhs_fn(fc), start=first, stop=last),
                         reads=[wb] + lhsT_bufs(kc, sub), writes=[pbb[sub]])
        for sub in range(4):
            zs = xs[:, sub, nt * 512:(nt + 1) * 512]
            if alpha_res:
                P.op("dve", lambda e, zs=zs, sub=sub: e.scalar_tensor_tensor(out=zs, in0=zs, scalar=DN_ALPHA, in1=pbanks[sub][:],
                                                                             op0=ALU.mult, op1=ALU.add),
                     reads=[pbb[sub]], writes=[xsb[sub]])
            else:
                P.op("act", lambda e, zs=zs, sub=sub: e.activation(out=zs, in_=pbanks[sub][:], func=AF.Copy),
                     reads=[pbb[sub]], writes=[xsb[sub]])


def load_and_transpose(C, x_dram, row0, xs, xsb, xT, xTb, ptr, ptrb, ident, nsub=4, KC=16):
    P = C.P
    for sub in range(nsub):
        P.dma("sp", lambda e, sub=sub: e.dma_start(out=xs[:, sub, :], in_=x_dram[row0 + sub * 128: row0 + (sub + 1) * 128, :]),
              writes=[xsb[sub]])
    k = 0
    for sub in range(nsub):
        for g in range(KC // 4):
            pt, ptb = ptr[k % len(ptr)], ptrb[k % len(ptr)]
            for j in range(4):
                kc = g * 4 + j
                P.op("pe", lambda e, pt=pt, j=j, kc=kc, sub=sub: e.transpose(out=pt[:, j, :], in_=xs[:, sub, kc * 128:(kc + 1) * 128],
                                                                            identity=ident[:]),
                     reads=[xsb[sub]], writes=[ptb])
            eng = "act" if k % 2 == 0 else "dve"
            dst = xT[:, g * 4:(g + 1) * 4, sub * 128:(sub + 1) * 128]
            if eng == "act":
                P.op("act", lambda e, pt=pt, dst=dst: e.activation(out=dst, in_=pt[:], func=AF.Copy), reads=[ptb], writes=[xTb])
            else:
                P.op("dve", lambda e, pt=pt, dst=dst: e.tensor_copy(out=dst, in_=pt[:]), reads=[ptb], writes=[xTb])
            k += 1


def load_consts(C, consts):
    P = C.P
    C.ident = C.sb([128, 128], F32, name="ident_sb")
    C.identB = Buf()
    P.dma("sp", lambda e: e.dma_start(out=C.ident[:], in_=consts["ident"]), writes=[C.identB])
    C.eps_ln = C.sb([128, 1], F32, name="eps_ln")
    C.eps_rms = C.sb([128, 1], F32, name="eps_rms")
    cb = Buf()
    C.c_negmean = C.sb([128, 1], F32, name="c_negmean")
    C.c_invd = C.sb([128, 1], F32, name="c_invd")
    P.op("dve", lambda e: e.memset(C.c_negmean[:], -1.0 / D), writes=[cb])
    P.op("dve", lambda e: e.memset(C.c_invd[:], 1.0 / D), writes=[cb])
    P.op("dve", lambda e: e.memset(C.eps_ln[:], LN_EPS), writes=[cb])
    P.op("dve", lambda e: e.memset(C.eps_rms[:], RMS_EPS), writes=[cb])
    C.constB = cb


def bcast_load(C, dst, vec_dram, buf):
    C.P.dma("sp", lambda e: e.dma_start(out=dst[:], in_=vec_dram.partition_broadcast(128)), writes=[buf])


def phase_mlp(C, x_in, w1, w2, g, b, x_out, xinB, xoutB):
    P = C.P
    with contextlib.ExitStack() as es:
        xs = C.sb([128, 4, D], F32, es=es)
        xT = C.sb([128, 16, 512], BF16, es=es)
        hT = C.sb([128, 64, 512], BF16, es=es)
        ring = PanelRing(C, 3, es)
        rst = [C.sb([128, 512], F32, es=es) for _ in range(2)]
        gb = C.sb([128, D], F32, es=es)
        bb = C.sb([128, D], F32, es=es)
        st = ln_stat_sets(C, es)
        pb = [C.ps([128, 512], F32, es=es) for _ in range(8)]
        pbB = [Buf() for _ in range(8)]
        gbB = Buf()
        bcast_load(C, gb, g, gbB)
        bcast_load(C, bb, b, gbB)
        xsb = [Buf() for _ in range(4)]
        xTb = Buf()
        rstB = [Buf() for _ in range(2)]
        hTb = [Buf() for _ in range(64)]
        ptr = [pb[4 + i][:].rearrange("p (a b) -> p a b", a=4) for i in range(4)]

        for tt in range(S // 512):
            ring.plan([panel_src(w1, 0, fg * 512) for fg in range(16)])
            ring.plan(stage_tm_srcs(w2, 64, D))
        for tt in range(S // 512):
            row0 = tt * 512
            for sub in range(4):
                P.dma("sp", lambda e, sub=sub, row0=row0: e.dma_start(out=xs[:, sub, :], in_=x_in[row0 + sub * 128: row0 + (sub + 1) * 128, :]),
                      reads=[xinB], writes=[xsb[sub]])
            k = 0
            for sub in range(4):
                for gq in range(4):
                    pt, ptb = ptr[k % 4], pbB[4 + k % 4]
                    for j in range(4):
                        kc = gq * 4 + j
                        P.op("pe", lambda e, pt=pt, j=j, kc=kc, sub=sub: e.transpose(out=pt[:, j, :], in_=xs[:, sub, kc * 128:(kc + 1) * 128],
                                                                                    identity=C.ident[:]),
                             reads=[xsb[sub], C.identB], writes=[ptb])
                    dst = xT[:, gq * 4:(gq + 1) * 4, sub * 128:(sub + 1) * 128]
                    if k % 2 == 0:
                        P.op("act", lambda e, pt=pt, dst=dst: e.activation(out=dst, in_=pt, func=AF.Copy), reads=[ptb], writes=[xTb])
                    else:
                        P.op("dve", lambda e, pt=pt, dst=dst: e.tensor_copy(out=dst, in_=pt), reads=[ptb], writes=[xTb])
                    k += 1
            k = 0
            for fg in range(16):
                wt, wb = ring.next()
                for fc in range(4):
                    bank, bankB = pb[4 + k % 4], pbB[4 + k % 4]
                    for kc in range(16):
                        P.op("pe", lambda e, wt=wt, kc=kc, fc=fc, bank=bank:
                             e.matmul(out=bank[:], lhsT=wt[:, kc * 512 + fc * 128: kc * 512 + (fc + 1) * 128], rhs=xT[:, kc, :],
                                      start=(kc == 0), stop=(kc == 15)),
                             reads=[wb, xTb], writes=[bankB])
                    r, rB = rst[k % 2], rstB[k % 2]
                    P.op("act", lambda e, r=r, bank=bank: e.activation(out=r[:], in_=bank[:], func=AF.Relu), reads=[bankB], writes=[rB])
                    f = fg * 4 + fc
                    P.op("dve", lambda e, r=r, f=f: e.tensor_tensor(out=hT[:, f, :], in0=r[:], in1=r[:], op=ALU.mult),
                         reads=[rB], writes=[hTb[f]])
                    k += 1
            stage_tm(C, ring, lambda kc, sub: hT[:, kc, sub * 128:(sub + 1) * 128], lambda kc, sub: [hTb[kc]], 64, w2, D,
                     xs, xsb, pb[0:4], pbB[0:4])
            def store(sub, row0=row0):
                P.dma("sp", lambda e, sub=sub, row0=row0: e.dma_start(out=x_out[row0 + sub * 128: row0 + (sub + 1) * 128, :], in_=xs[:, sub, :]),
                      reads=[xsb[sub]], writes=[xoutB], par=True)
            layernorm_tiles(C, [xs[:, sub, :] for sub in range(4)], xsb, gb, bb, gbB, st, store)
        P.barrier()


def build_xT_full(C, x_in, xinB, xT, xTb, xs, xsb, pbanks, pbB):
    P = C.P
    k = 0
    nsub = len(xsb)
    for tt in range(S // (128 * nsub)):
        row0 = tt * 128 * nsub
        for sub in range(nsub):
            P.dma("sp", lambda e, sub=sub, row0=row0: e.dma_start(out=xs[:, sub, :], in_=x_in[row0 + sub * 128: row0 + (sub + 1) * 128, :]),
                  reads=[xinB], writes=[xsb[sub]])
        for sub in range(nsub):
            for gq in range(4):
                pt = pbanks[k % len(pbanks)][:].rearrange("p (a b) -> p a b", a=4)
                ptb = pbB[k % len(pbanks)]
                for j in range(4):
                    kc = gq * 4 + j
                    P.op("pe", lambda e, pt=pt, j=j, kc=kc, sub=sub: e.transpose(out=pt[:, j, :], in_=xs[:, sub, kc * 128:(kc + 1) * 128],
                                                                                identity=C.ident[:]),
                         reads=[xsb[sub], C.identB], writes=[ptb])
                dst = xT[:, gq * 4:(gq + 1) * 4, row0 + sub * 128: row0 + (sub + 1) * 128]
                if k % 2 == 0:
                    P.op("act", lambda e, pt=pt, dst=dst: e.activation(out=dst, in_=pt, func=AF.Copy), reads=[ptb], writes=[xTb])
                else:
                    P.op("dve", lambda e, pt=pt, dst=dst: e.tensor_copy(out=dst, in_=pt), reads=[ptb], writes=[xTb])
                k += 1


def cols_src(W, r0, krows, colblocks):
    tot = sum(n for _, n in colblocks)

    def src(t):
        out = []
        view = t[:, 0:krows * tot].rearrange("p (k n) -> p k n", k=krows)
        o = 0
        for (c0, n) in colblocks:
            out.append((view[:, :, o:o + n], W[r0:r0 + krows * 128, c0:c0 + n].rearrange("(k p) n -> p k n", p=128)))
            o += n
        return out
    return src


def lin_fm(C, ring, xT, xTb, KC, width, banks, banksB, evac, ntok=S, kcnt=[0]):
    P = C.P
    wt, wb = ring.next()
    nfc = -(-width // 128)
    for fc in range(nfc):
        m = min(128, width - fc * 128)
        for tokt in range(ntok // 512):
            i = kcnt[0] % len(banks)
            kcnt[0] += 1
            bank, bankB = banks[i], banksB[i]
            for kc in range(KC):
                P.op("pe", lambda e, wt=wt, kc=kc, fc=fc, m=m, bank=bank, tokt=tokt:
                     e.matmul(out=bank[0:m, :], lhsT=wt[:, kc * width + fc * 128: kc * width + fc * 128 + m],
                              rhs=xT[:, kc, tokt * 512:(tokt + 1) * 512], start=(kc == 0), stop=(kc == KC - 1)),
                     reads=[wb, xTb], writes=[bankB])
            evac(fc, tokt, bank, bankB, m)


def lin_tm(C, ring, xT, xTb, KC, width, banks, banksB, evac, ntok=S, kcnt=[0]):
    P = C.P
    wt, wb = ring.next()
    for tokt in range(ntok // 128):
        i = kcnt[0] % len(banks)
        kcnt[0] += 1
        bank, bankB = banks[i], banksB[i]
        for kc in range(KC):
            P.op("pe", lambda e, wt=wt, kc=kc, bank=bank, tokt=tokt:
                 e.matmul(out=bank[:, 0:width], lhsT=xT[:, kc, tokt * 128:(tokt + 1) * 128], rhs=wt[:, kc * width:(kc + 1) * width],
                          start=(kc == 0), stop=(kc == KC - 1)),
                 reads=[wb, xTb], writes=[bankB])
        evac(tokt, bank, bankB)


def phase_mlstm(C, x_in, xinB, w_in, b_gates, conv_w, conv_b, norm_w, consts, v_s, so_s, hg_d, hgB):
    P = C.P
    QS = 128 ** -0.5
    vB, soB = Buf(), Buf()
    with contextlib.ExitStack() as es0:
        qT = C.sb([128, 8, S], BF16, es=es0)
        kT = C.sb([128, 8, S], BF16, es=es0)
        g_all = C.sb([128, 16, 16], F32, es=es0)
        qTb = [Buf() for _ in range(8)]
        kTb = [Buf() for _ in range(8)]
        gB = Buf()
        pb = [C.ps([128, 512], F32, es=es0) for _ in range(8)]
        pbB = [Buf() for _ in range(8)]
        with contextlib.ExitStack() as es:
            xT = C.sb([128, 16, S], BF16, es=es)
            xTb = Buf()
            xs = C.sb([128, 2, D], F32, es=es)
            xsb = [Buf() for _ in range(2)]
            ring = PanelRing(C, 2, es)
            cw = C.sb([128, 16, 5], F32, es=es)
            cwB = Buf()
            for j in range(4):
                P.dma("sp", lambda e, j=j: e.dma_start(out=cw[:, :, j], in_=conv_w[j].rearrange("(c p) -> p c", p=128), allow_slow_non_contiguous=True),
                      writes=[cwB], par=True)
            P.dma("sp", lambda e: e.dma_start(out=cw[:, :, 4], in_=conv_b.rearrange("(c p) -> p c", p=128), allow_slow_non_contiguous=True),
                  writes=[cwB], par=True)
            bgb = C.sb([128, 16], F32, es=es)
            bcast_load(C, bgb, b_gates, cwB)
            one_c = C.sb([128, 1], F32, es=es)
            P.op("dve", lambda e: e.memset(one_c[:], 1.0), writes=[cwB])
            build_xT_full(C, x_in, xinB, xT, xTb, xs, xsb, pb[4:8], pbB[4:8])
            ring.plan([panel_src(w_in, 0, p * 512) for p in range(4)])
            ring.plan([panel_src(w_in, 0, 2048 + p * 512) for p in range(8)])
            ring.plan([cols_src(w_in, 0, 16, [(6144, 16)])])
            raw = [C.sb([128, 3 + S], F32, es=es) for _ in range(2)]
            rawB = [Buf() for _ in range(2)]
            acc = [xs[:, 1, :]]
            accB = [xsb[1]]
            for r_ in range(2):
                P.op("pool", lambda e, r_=r_: e.memset(raw[r_][:, 0:3], 0.0), writes=[rawB[r_]])
            for pnl in range(4):
                def evac(fc, tokt, bank, bankB, m, pnl=pnl):
                    ci = pnl * 4 + fc
                    r, rB = raw[ci % 2], rawB[ci % 2]
                    P.op("act", lambda e, r=r, bank=bank, tokt=tokt: e.activation(out=r[:, 3 + tokt * 512: 3 + (tokt + 1) * 512], in_=bank[:],
                                                                                 func=AF.Copy), reads=[bankB], writes=[rB])
                    if tokt == 3:
                        a, aB = acc[0], accB[0]
                        P.op("dve", lambda e, a=a, r=r, ci=ci: e.tensor_scalar(out=a[:], in0=r[:, 3:3 + S], scalar1=cw[:, ci, 3:4], scalar2=cw[:, ci, 4:5],
                                                                              op0=ALU.mult, op1=ALU.add), reads=[rB, cwB], writes=[aB])
                        for j in (2, 1, 0):
                            eng = "dve"
                            P.op(eng, lambda e, a=a, r=r, ci=ci, j=j: e.scalar_tensor_tensor(out=a[:], in0=r[:, j:j + S], scalar=cw[:, ci, j:j + 1], in1=a[:],
                                                                                            op0=ALU.mult, op1=ALU.add), reads=[rB, cwB], writes=[aB])
                        if ci < 8:
                            P.op("act", lambda e, a=a: e.activation(out=a[:], in_=a[:], func=AF.Silu), reads=[], writes=[aB])
                            P.op("act", lambda e, a=a, ci=ci: e.activation(out=qT[:, ci, :], in_=a[:], func=AF.Copy, scale=QS),
                                 reads=[aB], writes=[qTb[ci]])
                        else:
                            P.op("act", lambda e, a=a, ci=ci: e.activation(out=kT[:, ci - 8, :], in_=a[:], func=AF.Silu), reads=[aB], writes=[kTb[ci - 8]])
                lin_fm(C, ring, xT, xTb, 16, 512, pb[0:4], pbB[0:4], evac)
            vst = [C.sb([128, 512], BF16, es=es) for _ in range(2)]
            vstB = [Buf() for _ in range(2)]
            ost = [C.sb([128, 512], F32, es=es) for _ in range(2)]
            ostB = [Buf() for _ in range(2)]
            cnt = [0]
            for pnl in range(8):
                def evac(tokt, bank, bankB, pnl=pnl):
                    i = cnt[0] % 2
                    cnt[0] += 1
                    if pnl < 4:
                        P.op("act", lambda e, i=i, bank=bank: e.activation(out=vst[i][:], in_=bank[:], func=AF.Copy), reads=[bankB], writes=[vstB[i]])
                        P.dma("sp", lambda e, i=i, tokt=tokt, pnl=pnl: e.dma_start(out=v_s[tokt * 128:(tokt + 1) * 128, pnl * 512:(pnl + 1) * 512], in_=vst[i][:]),
                              reads=[vstB[i]], writes=[vB], par=True)
                    else:
                        c0 = (pnl - 4) * 512
                        P.op("act", lambda e, i=i, bank=bank: e.activation(out=ost[i][:], in_=bank[:], func=AF.Sigmoid), reads=[bankB], writes=[ostB[i]])
                        P.dma("sp", lambda e, i=i, tokt=tokt, c0=c0: e.dma_start(out=so_s[tokt * 128:(tokt + 1) * 128, c0:c0 + 512], in_=ost[i][:]),
                              reads=[ostB[i]], writes=[soB], par=True)
                lin_tm(C, ring, xT, xTb, 16, 512, pb[0:4], pbB[0:4], evac)
            def evac_g(tokt, bank, bankB):
                P.op("dve", lambda e, bank=bank, tokt=tokt: e.tensor_tensor(out=g_all[:, tokt, :], in0=bank[:, 0:16], in1=bgb[:], op=ALU.add),
                     reads=[bankB, cwB], writes=[gB])
            lin_tm(C, ring, xT, xTb, 16, 16, pb[0:4], pbB[0:4], evac_g)
            ftmp = C.sb([128, 16, 8], F32, es=es)
            P.op("act", lambda e: e.activation(out=ftmp[:], in_=g_all[:, :, 8:16], func=AF.Exp, scale=-1.0), reads=[gB], writes=[gB])
            P.op("act", lambda e: e.activation(out=ftmp[:], in_=ftmp[:], func=AF.Ln, bias=one_c[:], scale=1.0), reads=[cwB], writes=[gB])
            P.op("dve", lambda e: e.tensor_scalar(out=g_all[:, :, 8:16], in0=ftmp[:], scalar1=-1.0, scalar2=None, op0=ALU.mult), reads=[], writes=[gB])
            P.barrier()
        with contextlib.ExitStack() as es:
            U = C.sb([128, 128], F32, es=es)
            LT = C.sb([128, 128], F32, es=es)
            MB = C.sb([128, 128], F32, es=es)
            ONE = C.sb([128, 128], F32, es=es)
            identb = C.sb([128, 128], BF16, es=es)
            oneb = C.sb([128, 1], BF16, es=es)
            cB = Buf()
            P.dma("sp", lambda e: e.dma_start(out=U[:], in_=consts["U"]), writes=[cB])
            P.dma("sp", lambda e: e.dma_start(out=LT[:], in_=consts["LT"]), writes=[cB])
            P.dma("sp", lambda e: e.dma_start(out=MB[:], in_=consts["MB"]), writes=[cB])
            P.op("dve", lambda e: e.memset(ONE[:], 1.0), writes=[cB])
            P.op("dve", lambda e: e.memset(oneb[:], 1.0), writes=[cB])
            P.op("dve", lambda e: e.tensor_copy(out=identb[:], in_=C.ident[:]), reads=[C.identB], writes=[cB])
            Cst = C.sb([128, 8, 257], F32, es=es)
            CB = [Buf() for _ in range(8)]
            P.op("pool", lambda e: e.memset(Cst[:], 0.0), writes=CB)
            R1 = C.sb([128, 8, 128], F32, es=es)
            R2 = C.sb([128, 8, 128], F32, es=es)
            RB = Buf()
            WT = C.sb([128, 8, 128], F32, es=es)
            EBs = C.sb([128, 8, 128], F32, es=es)
            WTB = [Buf() for _ in range(2)]
            EBB = [Buf() for _ in range(2)]
            swT = C.sb([128, 8, 128], BF16, es=es)
            swB = [Buf() for _ in range(2)]
            qs = C.sb([128, 8, 128], F32, es=es)
            qsB = [Buf() for _ in range(2)]
            kk = C.sb([128, 8, 128], BF16, es=es)
            kkB = [Buf() for _ in range(2)]
            vt = [C.sb([128, 8, 257], BF16, es=es) for _ in range(2)]
            vtB = [Buf() for _ in range(2)]
            vw = C.sb([128, 8, 257], BF16, es=es)
            vwB = [Buf() for _ in range(2)]
            sot = [C.sb([128, D], F32, es=es) for _ in range(2)]
            sotB = [Buf() for _ in range(2)]
            hh = C.sb([128, 8, 256], F32, es=es)
            hhB = [Buf() for _ in range(2)]
            hgt = [C.sb([128, D], BF16, es=es) for _ in range(2)]
            hgtB = [Buf() for _ in range(2)]
            den = C.sb([128, 8], F32, es=es)
            bst = C.sb([128, 8, 6], F32, es=es)
            mv = C.sb([128, 8, 2], F32, es=es)
            rstd = C.sb([128, 8], F32, es=es)
            nbias = C.sb([128, 8], F32, es=es)
            smB = [Buf() for _ in range(2)]
            nwb = C.sb([128, D], F32, es=es)
            nwB = Buf()
            bcast_load(C, nwb, norm_w, nwB)
            for i in range(2):
                P.op("pool", lambda e, i=i: e.memset(vt[i][:, :, 256:257], 1.0), writes=[vtB[i]])
            pD, pE, pS, pK, pN0, pN1, pC0, pC1 = pb
            bD, bE, bS, bK, bN0, bN1, bC0, bC1 = pbB
            pKb = pK[:].bitcast(BF16)
            WT2 = [WT, C.sb([128, 8, 128], F32, es=es)]
            EBs2 = [EBs, C.sb([128, 8, 128], F32, es=es)]
            swT2 = [swT, C.sb([128, 8, 128], BF16, es=es)]
            qs2 = [qs, C.sb([128, 8, 128], F32, es=es)]
            kk2 = [kk, C.sb([128, 8, 128], BF16, es=es)]
            vw2 = [vw, C.sb([128, 8, 257], BF16, es=es)]
            WTB2, EBB2, swB2, qsB2, kkB2, vwB2 = [[[Buf() for _ in range(2)] for _ in range(2)] for _ in range(6)]

            def prefetch(tt):
                c0 = tt * 128
                i2 = tt % 2
                P.dma("sp", lambda e, i2=i2, c0=c0: e.dma_start(out=vt[i2][:, :, 0:256], in_=v_s[c0:c0 + 128, :].rearrange("p (h v) -> p h v", h=8)),
                      reads=[vB], writes=[vtB[i2]])
                P.dma("sp", lambda e, i2=i2, c0=c0: e.dma_start(out=sot[i2][:], in_=so_s[c0:c0 + 128, :]), reads=[soB], writes=[sotB[i2]])

            def emit_R(tt):
                P.op("dve", lambda e, tt=tt: e.tensor_tensor(out=R1[:], in0=g_all[:, tt, 8:16].unsqueeze(2).broadcast_to([128, 8, 128]),
                                                            in1=LT[:].unsqueeze(1).broadcast_to([128, 8, 128]), op=ALU.mult), reads=[gB, cB], writes=[RB])
                P.op("pool", lambda e, tt=tt: e.tensor_tensor(out=R2[:], in0=g_all[:, tt, 0:8].unsqueeze(2).broadcast_to([128, 8, 128]),
                                                             in1=MB[:].unsqueeze(1).broadcast_to([128, 8, 128]), op=ALU.add), reads=[gB, cB], writes=[RB])

            def pre(tt, hg):
                c0 = tt * 128
                i2 = tt % 2
                par = tt % 2
                WT, EBs, swT, qs, kk, vw = WT2[par], EBs2[par], swT2[par], qs2[par], kk2[par], vw2[par]
                WTB, EBB, swB, qsB, kkB, vwB = WTB2[par], EBB2[par], swB2[par], qsB2[par], kkB2[par], vwB2[par]
                hs = slice(hg * 4, hg * 4 + 4)
                hs = slice(hg * 4, hg * 4 + 4)
                R1g = R1[:, hs, :].rearrange("p a b -> p (a b)")
                R2g = R2[:, hs, :].rearrange("p a b -> p (a b)")
                P.op("pe", lambda e, R1g=R1g: e.matmul(out=pD[:], lhsT=U[:], rhs=R1g, start=True, stop=False), reads=[RB, cB], writes=[bD])
                P.op("pe", lambda e, R2g=R2g: e.matmul(out=pD[:], lhsT=C.ident[:], rhs=R2g, start=False, stop=True), reads=[RB, C.identB], writes=[bD])
                P.op("pe", lambda e, R1g=R1g: e.matmul(out=pE[:], lhsT=ONE[:], rhs=R1g, start=True, stop=True), reads=[RB, cB], writes=[bE])
                WTg = WT[:, hs, :]
                EBg = EBs[:, hs, :]
                P.op("act", lambda e, WTg=WTg: e.activation(out=WTg.rearrange("p a b -> p (a b)"), in_=pD[:], func=AF.Exp), reads=[bD], writes=[WTB[hg]])
                P.op("act", lambda e, EBg=EBg: e.activation(out=EBg.rearrange("p a b -> p (a b)"), in_=pE[:], func=AF.Exp), reads=[bE], writes=[EBB[hg]])
                for h4 in range(4):
                    h = hg * 4 + h4
                    P.op("pe", lambda e, h=h, h4=h4, c0=c0: e.matmul(out=pS[:, h4 * 128:(h4 + 1) * 128], lhsT=kT[:, h, c0:c0 + 128], rhs=qT[:, h, c0:c0 + 128],
                                                                    start=True, stop=True), reads=[kTb[h], qTb[h]], writes=[bS])
                swg = swT[:, hs, :]
                P.op("dve", lambda e, swg=swg, WTg=WTg: e.tensor_tensor(out=swg.rearrange("p a b -> p (a b)"), in0=pS[:], in1=WTg.rearrange("p a b -> p (a b)"), op=ALU.mult),
                     reads=[bS, WTB[hg]], writes=[swB[hg]])
                qsg = qs[:, hs, :]
                P.op("dve", lambda e, qsg=qsg, EBg=EBg, hs=hs, c0=c0: e.tensor_tensor(out=qsg, in0=qT[:, hs, c0:c0 + 128], in1=EBg, op=ALU.mult),
                     reads=[EBB[hg]] + qTb[hg * 4:hg * 4 + 4], writes=[qsB[hg]])
                for h4 in range(4):
                    h = hg * 4 + h4
                    P.op("pe", lambda e, h=h, h4=h4, c0=c0: e.transpose(out=pKb[:, h4 * 128:(h4 + 1) * 128], in_=kT[:, h, c0:c0 + 128], identity=identb[:]),
                         reads=[kTb[h], cB], writes=[bK])
                kkg = kk[:, hs, :]
                P.op("act", lambda e, kkg=kkg: e.activation(out=kkg.rearrange("p a b -> p (a b)"), in_=pKb[:, 0:512], func=AF.Copy), reads=[bK], writes=[kkB[hg]])
                vwg = vw[:, hs, :]
                P.op("pool", lambda e, vwg=vwg, i2=i2, hs=hs: e.tensor_tensor(out=vwg, in0=vt[i2][:, hs, :], in1=WT[:, hs, 127:128].broadcast_to([128, 4, 257]), op=ALU.mult),
                     reads=[vtB[i2], WTB[hg]], writes=[vwB[hg]])

            def post(tt, hg):
                c0 = tt * 128
                i2 = tt % 2
                par = tt % 2
                WT, EBs, swT, qs, kk, vw = WT2[par], EBs2[par], swT2[par], qs2[par], kk2[par], vw2[par]
                WTB, EBB, swB, qsB, kkB, vwB = WTB2[par], EBB2[par], swB2[par], qsB2[par], kkB2[par], vwB2[par]
                hs = slice(hg * 4, hg * 4 + 4)
                pN = [pN0, pN1]
                bN = [bN0, bN1]
                for h4 in range(4):
                    h = hg * 4 + h4
                    o_ = pN[h4 // 2][:, (h4 % 2) * 256:(h4 % 2) * 256 + 256]
                    P.op("pe", lambda e, h=h, o_=o_, i2=i2: e.matmul(out=o_, lhsT=swT[:, h, :], rhs=vt[i2][:, h, 0:256], start=True, stop=False),
                         reads=[swB[hg], vtB[i2]], writes=[bN[h4 // 2]])
                    P.op("pe", lambda e, h=h, o_=o_: e.matmul(out=o_, lhsT=qs[:, h, :], rhs=Cst[:, h, 0:256], start=False, stop=True),
                         reads=[qsB[hg], CB[h]], writes=[bN[h4 // 2]])
                for h4 in range(4):
                    h = hg * 4 + h4
                    d_ = pK[:, 256 + h4: 256 + h4 + 1]
                    P.op("pe", lambda e, h=h, d_=d_: e.matmul(out=d_, lhsT=swT[:, h, :], rhs=oneb[:], start=True, stop=False),
                         reads=[swB[hg], cB, kkB[hg]], writes=[bK])
                    P.op("pe", lambda e, h=h, d_=d_: e.matmul(out=d_, lhsT=qs[:, h, :], rhs=Cst[:, h, 256:257], start=False, stop=True),
                         reads=[qsB[hg], CB[h]], writes=[bK])
                pC = [pC0, pC1]
                bC = [bC0, bC1]
                for h4 in range(4):
                    h = hg * 4 + h4
                    o_ = pC[h4 // 2][:, (h4 % 2) * 256:(h4 % 2) * 256 + 256]
                    P.op("pe", lambda e, h=h, o_=o_: e.matmul(out=o_, lhsT=kk[:, h, :], rhs=vw[:, h, 0:256], start=True, stop=True),
                         reads=[kkB[hg], vwB[hg]], writes=[bC[h4 // 2]])
                for h4 in range(4):
                    h = hg * 4 + h4
                    d_ = pK[:, 264 + h4: 264 + h4 + 1]
                    P.op("pe", lambda e, h=h, d_=d_: e.matmul(out=d_, lhsT=kk[:, h, :], rhs=vw[:, h, 256:257], start=True, stop=True),
                         reads=[kkB[hg], vwB[hg]], writes=[bK])
                deng = den[:, hs]
                P.op("act", lambda e, deng=deng: e.activation(out=deng, in_=pK[:, 256:260], func=AF.Abs), reads=[bK], writes=[smB[hg]])
                P.op("dve", lambda e, deng=deng: e.tensor_scalar(out=deng, in0=deng, scalar1=1.0, scalar2=None, op0=ALU.max), reads=[], writes=[smB[hg]])
                P.op("dve", lambda e, deng=deng: e.reciprocal(out=deng, in_=deng), reads=[], writes=[smB[hg]])
                for h4 in range(4):
                    h = hg * 4 + h4
                    o_ = pN[h4 // 2][:, (h4 % 2) * 256:(h4 % 2) * 256 + 256]
                    P.op("act", lambda e, h=h, o_=o_: e.activation(out=hh[:, h, :], in_=o_, func=AF.Identity, scale=den[:, h:h + 1]),
                         reads=[bN[h4 // 2], smB[hg]], writes=[hhB[hg]])
                for h4 in range(4):
                    h = hg * 4 + h4
                    o_ = pC[h4 // 2][:, (h4 % 2) * 256:(h4 % 2) * 256 + 256]
                    P.op("dve", lambda e, h=h, o_=o_: e.scalar_tensor_tensor(out=Cst[:, h, 0:256], in0=Cst[:, h, 0:256], scalar=EBs[:, h, 127:128], in1=o_,
                                                                            op0=ALU.mult, op1=ALU.add), reads=[bC[h4 // 2], EBB[hg]], writes=[CB[h]])
                    P.op("dve", lambda e, h=h, h4=h4: e.scalar_tensor_tensor(out=Cst[:, h, 256:257], in0=Cst[:, h, 256:257], scalar=EBs[:, h, 127:128],
                                                                            in1=pK[:, 264 + h4:265 + h4], op0=ALU.mult, op1=ALU.add),
                         reads=[bK, EBB[hg]], writes=[CB[h]])
                for h4 in range(4):
                    h = hg * 4 + h4
                    P.op("dve", lambda e, h=h: e.bn_stats(out=bst[:, h, :], in_=hh[:, h, :]), reads=[hhB[hg]], writes=[smB[hg]])
                    P.op("dve", lambda e, h=h: e.bn_aggr(out=mv[:, h, :], in_=bst[:, h, :]), reads=[], writes=[smB[hg]])
                rg = rstd[:, hs]
                P.op("act", lambda e, rg=rg, hs=hs: e.activation(out=rg, in_=mv[:, hs, 1], func=AF.Ln, bias=C.eps_ln[:], scale=1.0), reads=[C.constB], writes=[smB[hg]])
                P.op("act", lambda e, rg=rg: e.activation(out=rg, in_=rg, func=AF.Exp, scale=-0.5), reads=[], writes=[smB[hg]])
                nbg = nbias[:, hs]
                P.op("dve", lambda e, nbg=nbg, rg=rg, hs=hs: e.scalar_tensor_tensor(out=nbg, in0=mv[:, hs, 0], scalar=-1.0, in1=rg, op0=ALU.mult, op1=ALU.mult),
                     reads=[], writes=[smB[hg]])
                for h4 in range(4):
                    h = hg * 4 + h4
                    P.op("act", lambda e, h=h: e.activation(out=hh[:, h, :], in_=hh[:, h, :], func=AF.Identity, bias=nbias[:, h:h + 1], scale=rstd[:, h:h + 1]),
                         reads=[smB[hg]], writes=[hhB[hg]])
                hflat = hh[:, hg * 4:hg * 4 + 4, :].rearrange("p a b -> p (a b)")
                P.op("pool", lambda e, hflat=hflat, hg=hg: e.tensor_tensor(out=hflat, in0=hflat, in1=nwb[:, hg * 1024:(hg + 1) * 1024], op=ALU.mult),
                     reads=[nwB], writes=[hhB[hg]])
                P.op("dve", lambda e, i2=i2, hg=hg, hflat=hflat: e.tensor_tensor(out=hgt[i2][:, hg * 1024:(hg + 1) * 1024], in0=hflat,
                                                                                in1=sot[i2][:, hg * 1024:(hg + 1) * 1024], op=ALU.mult),
                     reads=[hhB[hg], sotB[i2]], writes=[hgtB[i2]])

            prefetch(0)
            prefetch(1)
            emit_R(0)
            pre(0, 0)
            pre(0, 1)
            for tt in range(16):
                c0 = tt * 128
                i2 = tt % 2
                if tt + 1 < 16:
                    emit_R(tt + 1)
                    pre(tt + 1, 0)
                    pre(tt + 1, 1)
                post(tt, 0)
                post(tt, 1)
                P.dma("sp", lambda e, i2=i2, c0=c0: e.dma_start(out=hg_d[c0:c0 + 128, :], in_=hgt[i2][:]), reads=[hgtB[i2]], writes=[hgB], par=True)
                if tt + 2 < 16:
                    prefetch(tt + 2)
            P.barrier()


def phase_out(C, a_d, aB, fm, W, x_res, xresB, g, b, x_out, xoutB):
    P = C.P
    with contextlib.ExitStack() as es:
        xs2 = [C.sb([128, 4, D], F32, es=es) for _ in range(2)]
        xsb2 = [[Buf() for _ in range(4)] for _ in range(2)]
        aT2 = [C.sb([128, 16, 512], BF16, es=es) for _ in range(2)]
        aTb2 = [Buf() for _ in range(2)]
        at = C.sb([128, 4, D], BF16, es=es) if not fm else None
        atB = [Buf() for _ in range(4)]
        identb = C.sb([128, 128], BF16, es=es)
        idB = Buf()
        P.op("dve", lambda e: e.tensor_copy(out=identb[:], in_=C.ident[:]), reads=[C.identB], writes=[idB])
        wres = C.sb([128, 16, D], BF16, es=es)
        wresB = [Buf() for _ in range(4)]
        for nt in range(4):
            P.dma("pool", lambda e, nt=nt: e.dma_start(out=wres[:, :, nt * 512:(nt + 1) * 512],
                                                       in_=W[:, nt * 512:(nt + 1) * 512].rearrange("(k p) n -> p k n", p=128)), writes=[wresB[nt]])
        gb = C.sb([128, D], F32, es=es)
        bb = C.sb([128, D], F32, es=es)
        gbB = Buf()
        bcast_load(C, gb, g, gbB)
        bcast_load(C, bb, b, gbB)
        st2 = ln_act_sets(C, es, 2)
        pb = [C.ps([128, 512], F32, es=es) for _ in range(6)]
        pbB = [Buf() for _ in range(6)]
        def prep(tt):
            row0 = tt * 512
            xs, xsb, aT, aTb = xs2[tt % 2], xsb2[tt % 2], aT2[tt % 2], aTb2[tt % 2]
            for sub in range(4):
                P.dma("sp", lambda e, sub=sub, row0=row0, xs=xs: e.dma_start(out=xs[:, sub, :], in_=x_res[row0 + sub * 128: row0 + (sub + 1) * 128, :]),
                      reads=[xresB], writes=[xsb[sub]])
            if fm:
                P.dma("sp", lambda e, row0=row0, aT=aT: e.dma_start(out=aT[:], in_=a_d[:, row0:row0 + 512].rearrange("(k p) n -> p k n", p=128)),
                      reads=[aB], writes=[aTb])
            else:
                for sub in range(4):
                    P.dma("sp", lambda e, sub=sub, row0=row0: e.dma_start(out=at[:, sub, :], in_=a_d[row0 + sub * 128: row0 + (sub + 1) * 128, :]),
                          reads=[aB], writes=[atB[sub]])
                k = 0
                for sub in range(4):
                    for g8 in range(2):
                        pt = pb[4 + k % 2][:].bitcast(BF16).rearrange("p (a b) -> p a b", a=8)
                        ptb = pbB[4 + k % 2]
                        for j in range(8):
                            kc = g8 * 8 + j
                            P.op("pe", lambda e, pt=pt, j=j, kc=kc, sub=sub: e.transpose(out=pt[:, j, :], in_=at[:, sub, kc * 128:(kc + 1) * 128], identity=identb[:]),
                                 reads=[atB[sub], idB], writes=[ptb])
                        dst = aT[:, g8 * 8:(g8 + 1) * 8, sub * 128:(sub + 1) * 128]
                        if k % 2 == 0:
                            P.op("act", lambda e, pt=pt, dst=dst: e.activation(out=dst, in_=pt, func=AF.Copy), reads=[ptb], writes=[aTb])
                        else:
                            P.op("dve", lambda e, pt=pt, dst=dst: e.tensor_copy(out=dst, in_=pt), reads=[ptb], writes=[aTb])
                        k += 1
        prep(0)
        for tt in range(S // 512):
            row0 = tt * 512
            xs, xsb, aT, aTb = xs2[tt % 2], xsb2[tt % 2], aT2[tt % 2], aTb2[tt % 2]
            if tt + 1 < S // 512:
                prep(tt + 1)
            stage_tm(C, None, lambda kc, sub, aT=aT: aT[:, kc, sub * 128:(sub + 1) * 128], lambda kc, sub, aTb=aTb: [aTb], 16, W, D, xs, xsb, pb[0:4], pbB[0:4],
                     wres=wres, wresB=wresB)
            def store(sub, row0=row0, xs=xs, xsb=xsb):
                P.dma("sp", lambda e, sub=sub, row0=row0, xs=xs: e.dma_start(out=x_out[row0 + sub * 128: row0 + (sub + 1) * 128, :], in_=xs[:, sub, :]),
                      reads=[xsb[sub]], writes=[xoutB], par=True)
            layernorm_tiles_act(C, [xs[:, sub, :] for sub in range(4)], xsb, gb, bb, gbB, st2, store)
        P.barrier()


def host_consts():
    j = np.arange(128)[:, None]
    t = np.arange(128)[None, :]
    return {
        "ident": np.eye(128, dtype=np.float32),
        "U": (j > t).astype(np.float32),
        "LT": (j <= t).astype(np.float32),
        "MB": np.where(j > t, -30000.0, 0.0).astype(np.float32),
    }


def rms_norm_fm(C, cT, cTb, cnT, cnTb, gn, gnB, ONE, oneB, sqst, sqB, rs, rsB, bank, bankB):
    P = C.P
    for tokt in range(4):
        ts_ = slice(tokt * 512, (tokt + 1) * 512)
        for fc in range(4):
            i = fc % 2
            P.op("pool", lambda e, i=i, fc=fc, ts_=ts_: e.tensor_tensor(out=sqst[i][:], in0=cT[:, fc, ts_], in1=cT[:, fc, ts_], op=ALU.mult),
                 reads=[cTb], writes=[sqB[i]])
            P.op("pe", lambda e, i=i, fc=fc: e.matmul(out=bank[:], lhsT=ONE[:], rhs=sqst[i][:], start=(fc == 0), stop=(fc == 3)),
                 reads=[sqB[i], oneB], writes=[bankB])
        P.op("act", lambda e: e.activation(out=rs[:], in_=bank[:], func=AF.Ln, bias=C.eps_rms[:], scale=1.0 / 512), reads=[bankB, C.constB], writes=[rsB])
        P.op("act", lambda e: e.activation(out=rs[:], in_=rs[:], func=AF.Exp, scale=-0.5), reads=[], writes=[rsB])
        for fc in range(4):
            P.op("dve", lambda e, fc=fc, ts_=ts_: e.scalar_tensor_tensor(out=cnT[:, fc, ts_], in0=cT[:, fc, ts_], scalar=gn[:, fc:fc + 1], in1=rs[:],
                                                                        op0=ALU.mult, op1=ALU.mult), reads=[cTb, gnB, rsB], writes=[cnTb])


def phase_kvq(C, x_in, xinB, w_down, kv_norm_w, w_up, w_dq, q_norm_w, w_uq, consts, knT_d, krT_d, v_d, qnT_d, qrT_d, kvB):
    P = C.P
    with contextlib.ExitStack() as es:
        xT = C.sb([128, 16, S], BF16, es=es)
        xTb = Buf()
        pb = [C.ps([128, 512], F32, es=es) for _ in range(8)]
        pbB = [Buf() for _ in range(8)]
        with contextlib.ExitStack() as es1:
            xs = C.sb([128, 4, D], F32, es=es1)
            xsb = [Buf() for _ in range(4)]
            build_xT_full(C, x_in, xinB, xT, xTb, xs, xsb, pb[4:8], pbB[4:8])
            P.barrier()
        ring = PanelRing(C, 2, es)
        cT = C.sb([128, 4, S], F32, es=es)
        cTb = Buf()
        cnT = C.sb([128, 4, S], BF16, es=es)
        cnTb = Buf()
        cos2 = C.sb([128, S], F32, es=es)
        sinS = C.sb([128, S], F32, es=es)
        tabB = Buf()
        P.dma("sp", lambda e: e.dma_start(out=cos2[:], in_=consts["cos2"]), writes=[tabB], par=True)
        P.dma("sp", lambda e: e.dma_start(out=sinS[:], in_=consts["sinS"]), writes=[tabB], par=True)
        gn = C.sb([128, 8], F32, es=es)
        gnB = Buf()
        P.dma("sp", lambda e: e.dma_start(out=gn[:, 0:4], in_=kv_norm_w.rearrange("(c p) -> p c", p=128), allow_slow_non_contiguous=True), writes=[gnB], par=True)
        P.dma("sp", lambda e: e.dma_start(out=gn[:, 4:8], in_=q_norm_w.rearrange("(c p) -> p c", p=128), allow_slow_non_contiguous=True), writes=[gnB], par=True)
        ONE = C.sb([128, 128], F32, es=es)
        oneB = Buf()
        P.op("dve", lambda e: e.memset(ONE[:], 1.0), writes=[oneB])
        sqst = [C.sb([128, 512], F32, es=es) for _ in range(2)]
        sqB = [Buf() for _ in range(2)]
        rs = C.sb([128, 512], F32, es=es)
        rsB = Buf()
        kr1 = C.sb([128, S], F32, es=es)
        kr1B = Buf()
        krb = C.sb([128, S], BF16, es=es)
        krbB = Buf()
        tmp = [C.sb([128, 512], F32, es=es) for _ in range(2)]
        tmpB = [Buf() for _ in range(2)]
        ost = [C.sb([128, 512], BF16, es=es) for _ in range(4)]
        ostB = [Buf() for _ in range(4)]
        cnt = [0]
        plan = [panel_src(w_down, 0, 0),
                cols_src(w_down, 0, 16, [(512, 64), (512, 64)]),
                cols_src(w_down, 0, 16, [(544, 32), (512, 32), (544, 32), (512, 32)])]
        for pg in range(4):
            plan.append(cols_src(w_up, 0, 4, [(h * 256, 128) for h in range(pg * 4, pg * 4 + 4)]))
        for pg in range(4):
            plan.append(cols_src(w_up, 0, 4, [(h * 256 + 128, 128) for h in range(pg * 4, pg * 4 + 4)]))
        plan.append(panel_src(w_dq, 0, 0))
        for pg in range(4):
            plan.append(cols_src(w_uq, 0, 4, [(h * 192, 128) for h in range(pg * 4, pg * 4 + 4)]))
        for pr in range(8):
            h0, h1 = 2 * pr, 2 * pr + 1
            plan.append(cols_src(w_uq, 0, 4, [(h0 * 192 + 128, 64), (h1 * 192 + 128, 64)]))
            plan.append(cols_src(w_uq, 0, 4, [(h0 * 192 + 160, 32), (h0 * 192 + 128, 32), (h1 * 192 + 160, 32), (h1 * 192 + 128, 32)]))
        ring.plan(plan)

        def store(bank, bankB, dst_fn, m=128, eng_i=[0]):
            i = cnt[0] % 4
            cnt[0] += 1
            if i % 2 == 0:
                P.op("act", lambda e, i=i, bank=bank: e.activation(out=ost[i][:], in_=bank[:], func=AF.Copy), reads=[bankB], writes=[ostB[i]])
            else:
                P.op("dve", lambda e, i=i, bank=bank: e.tensor_copy(out=ost[i][:], in_=bank[:]), reads=[bankB], writes=[ostB[i]])
            P.dma("sp", lambda e, i=i: e.dma_start(out=dst_fn(), in_=ost[i][:]), reads=[ostB[i]], writes=[kvB], par=True)

        def ev_c(fc, tokt, bank, bankB, m):
            P.op("act", lambda e, fc=fc, tokt=tokt, bank=bank: e.activation(out=cT[:, fc, tokt * 512:(tokt + 1) * 512], in_=bank[:], func=AF.Copy),
                 reads=[bankB], writes=[cTb])
        lin_fm(C, ring, xT, xTb, 16, 512, pb[0:4], pbB[0:4], ev_c)

        def rope_pair(src, srcb, KC, dst_fn):
            def ev1(fc, tokt, bank, bankB, m):
                P.op("dve", lambda e, tokt=tokt, bank=bank: e.tensor_tensor(out=kr1[:, tokt * 512:(tokt + 1) * 512], in0=bank[:], in1=cos2[:, tokt * 512:(tokt + 1) * 512],
                                                                           op=ALU.mult), reads=[bankB, tabB], writes=[kr1B])
            lin_fm(C, ring, src, srcb, KC, 128, pb[0:4], pbB[0:4], ev1)

            def ev2(fc, tokt, bank, bankB, m):
                i = tokt % 2
                P.op("dve", lambda e, i=i, tokt=tokt, bank=bank: e.tensor_tensor(out=tmp[i][:], in0=bank[:], in1=sinS[:, tokt * 512:(tokt + 1) * 512], op=ALU.mult),
                     reads=[bankB, tabB], writes=[tmpB[i]])
                dst_fn(i, tokt)
            lin_fm(C, ring, src, srcb, KC, 128, pb[0:4], pbB[0:4], ev2)

        def kr_dst(i, tokt):
            P.op("pool", lambda e, i=i, tokt=tokt: e.tensor_tensor(out=krb[:, tokt * 512:(tokt + 1) * 512], in0=kr1[:, tokt * 512:(tokt + 1) * 512], in1=tmp[i][:], op=ALU.add),
                 reads=[kr1B, tmpB[i]], writes=[krbB])
        rope_pair(xT, xTb, 16, kr_dst)
        P.dma("sp", lambda e: e.dma_start(out=krT_d, in_=krb[:]), reads=[krbB], writes=[kvB], par=True)

        rms_norm_fm(C, cT, cTb, cnT, cnTb, gn[:, 0:4], gnB, ONE, oneB, sqst, sqB, rs, rsB, pb[4], pbB[4])
        for pg in range(4):
            def ev_kn(fc, tokt, bank, bankB, m, pg=pg):
                h = pg * 4 + fc
                store(bank, bankB, lambda h=h, tokt=tokt: knT_d[h, :, tokt * 512:(tokt + 1) * 512])
            lin_fm(C, ring, cnT, cnTb, 4, 512, pb[0:4], pbB[0:4], ev_kn)
        for pg in range(4):
            def ev_v(tokt, bank, bankB, pg=pg):
                store(bank, bankB, lambda pg=pg, tokt=tokt: v_d[pg * 4:(pg + 1) * 4, tokt * 128:(tokt + 1) * 128, :].rearrange("h t v -> t h v"))
            lin_tm(C, ring, cnT, cnTb, 4, 512, pb[0:4], pbB[0:4], ev_v)

        lin_fm(C, ring, xT, xTb, 16, 512, pb[0:4], pbB[0:4], ev_c)
        rms_norm_fm(C, cT, cTb, cnT, cnTb, gn[:, 4:8], gnB, ONE, oneB, sqst, sqB, rs, rsB, pb[4], pbB[4])
        for pg in range(4):
            def ev_qn(fc, tokt, bank, bankB, m, pg=pg):
                h = pg * 4 + fc
                store(bank, bankB, lambda h=h, tokt=tokt: qnT_d[h, :, tokt * 512:(tokt + 1) * 512])
            lin_fm(C, ring, cnT, cnTb, 4, 512, pb[0:4], pbB[0:4], ev_qn)
        for pr in range(8):
            def q_dst(i, tokt, pr=pr):
                j = cnt[0] % 4
                cnt[0] += 1
                P.op("pool", lambda e, i=i, j=j, tokt=tokt: e.tensor_tensor(out=ost[j][:], in0=kr1[:, tokt * 512:(tokt + 1) * 512], in1=tmp[i][:], op=ALU.add),
                     reads=[kr1B, tmpB[i]], writes=[ostB[j]])
                P.dma("sp", lambda e, j=j, pr=pr, tokt=tokt: e.dma_start(out=qrT_d[pr, :, tokt * 512:(tokt + 1) * 512], in_=ost[j][:]),
                      reads=[ostB[j]], writes=[kvB], par=True)
            rope_pair(cnT, cnTb, 4, q_dst)
        P.barrier()


def phase_att(C, knT_d, krT_d, v_d, qnT_d, qrT_d, kvB, oT_d, oTB):
    P = C.P
    SCALE = 192 ** -0.5
    with contextlib.ExitStack() as es:
        kr = C.sb([128, S], BF16, es=es)
        krB = Buf()
        P.dma("sp", lambda e: e.dma_start(out=kr[:], in_=krT_d), reads=[kvB], writes=[krB])
        kn = [C.sb([128, S], BF16, es=es) for _ in range(2)]
        qn = [C.sb([128, S], BF16, es=es) for _ in range(2)]
        qr = [C.sb([128, S], BF16, es=es) for _ in range(2)]
        vv = [C.sb([128, 16, 128], BF16, es=es) for _ in range(2)]
        hB = [Buf() for _ in range(2)]
        onesb = C.sb([128, 128], BF16, es=es)
        oB = Buf()
        P.op("dve", lambda e: e.memset(onesb[:], 1.0), writes=[oB])
        pT = [C.sb([128, 512], BF16, es=es) for _ in range(4)]
        pTB = [Buf() for _ in range(4)]
        rsum = C.sb([128, 512], F32, es=es)
        rsB = Buf()
        ot = [C.sb([128, 512], BF16, es=es) for _ in range(2)]
        otB = [Buf() for _ in range(2)]
        pS = [C.ps([128, 512], F32, es=es) for _ in range(3)]
        pSB = [Buf() for _ in range(3)]
        pO = [C.ps([128, 512], F32, es=es) for _ in range(2)]
        pOB = [Buf() for _ in range(2)]
        pM = [C.ps([128, 512], F32, es=es) for _ in range(2)]
        pMB = [Buf() for _ in range(2)]

        def load_head(h):
            i = h % 2
            P.dma("sp", lambda e, i=i, h=h: e.dma_start(out=kn[i][:], in_=knT_d[h]), reads=[kvB], writes=[hB[i]])
            P.dma("sp", lambda e, i=i, h=h: e.dma_start(out=qn[i][:], in_=qnT_d[h]), reads=[kvB], writes=[hB[i]], par=True)
            P.dma("sp", lambda e, i=i, h=h: e.dma_start(out=qr[i][:], in_=qrT_d[h // 2]), reads=[kvB], writes=[hB[i]], par=True)
            P.dma("sp", lambda e, i=i, h=h: e.dma_start(out=vv[i][:], in_=v_d[h].rearrange("(t p) v -> p t v", p=128)), reads=[kvB], writes=[hB[i]], par=True)
        load_head(0)
        blk = 0
        oc = 0
        pend = None

        def flush():
            nonlocal pend
            if pend is not None:
                pend()
                pend = None
        for h in range(16):
            i = h % 2
            flush()
            if h + 1 < 16:
                load_head(h + 1)
            r0 = (h % 2) * 64
            for qt in range(4):
                po, poB = pO[oc % 2], pOB[oc % 2]
                pm, pmB = pM[oc % 2], pMB[oc % 2]
                nk = 4 * qt + 4
                for kt in range(nk):
                    j = kt - 4 * qt
                    c0 = max(0, j) * 128
                    n = 512 - c0
                    ps_, psB = pS[blk % 3], pSB[blk % 3]
                    pt, ptB = pT[blk % 4], pTB[blk % 4]
                    blk += 1
                    q0 = qt * 512 + c0
                    P.op("pe", lambda e, i=i, kt=kt, q0=q0, n=n, ps_=ps_: e.matmul(out=ps_[:, 0:n], lhsT=kn[i][:, kt * 128:(kt + 1) * 128], rhs=qn[i][:, q0:q0 + n],
                                                                                  start=True, stop=False), reads=[hB[i]], writes=[psB])
                    P.op("pe", lambda e, i=i, kt=kt, q0=q0, n=n, ps_=ps_, r0=r0: e.matmul(out=ps_[:, 0:n], lhsT=kr[r0:r0 + 64, kt * 128:(kt + 1) * 128],
                                                                                         rhs=qr[i][r0:r0 + 64, q0:q0 + n], start=False, stop=True),
                         reads=[hB[i], krB], writes=[psB])
                    P.op("act", lambda e, pt=pt, ps_=ps_, n=n: e.activation(out=pt[:, 0:n], in_=ps_[:, 0:n], func=AF.Exp, scale=SCALE), reads=[psB], writes=[ptB])
                    if j >= 0:
                        P.op("pool", lambda e, pt=pt: e.memset(pt[64:128, 0:64], 0.0), reads=[], writes=[ptB])
                    flush()

                    def pv(i=i, kt=kt, pt=pt, ptB=ptB, po=po, poB=poB, pm=pm, pmB=pmB, c0=c0, n=n, nk=nk):
                        P.op("pe", lambda e: e.matmul(out=po[:, c0:c0 + n], lhsT=vv[i][:, kt, :], rhs=pt[:, 0:n], start=(kt == 0), stop=(kt == nk - 1)),
                             reads=[hB[i], ptB], writes=[poB])
                        P.op("pe", lambda e: e.matmul(out=pm[:, c0:c0 + n], lhsT=onesb[:], rhs=pt[:, 0:n], start=(kt == 0), stop=(kt == nk - 1)),
                             reads=[oB, ptB], writes=[pmB])
                    pend = pv
                    if kt == nk - 1:
                        def fin(po=po, poB=poB, pm=pm, pmB=pmB, h=h, qt=qt, oc=oc, pv=pv):
                            pv()
                            P.op("dve", lambda e: e.reciprocal(out=rsum[:], in_=pm[:]), reads=[pmB], writes=[rsB])
                            o_, o_B = ot[oc % 2], otB[oc % 2]
                            P.op("dve", lambda e: e.tensor_tensor(out=o_[:], in0=po[:], in1=rsum[:], op=ALU.mult), reads=[poB, rsB], writes=[o_B])
                            P.dma("sp", lambda e: e.dma_start(out=oT_d[h * 128:(h + 1) * 128, qt * 512:(qt + 1) * 512], in_=o_[:]),
                                  reads=[o_B], writes=[oTB], par=True)
                        pend = fin
                oc += 1
        flush()
        P.barrier()


def rope_tables():
    inv_freq = (np.float32(10000.0) ** (-np.arange(0, 64, 2, dtype=np.float32) / np.float32(64))).astype(np.float32)
    ang = np.arange(S, dtype=np.float32)[:, None] * inv_freq[None, :]
    cos = np.cos(ang.astype(np.float64)).astype(np.float32).T
    sin = np.sin(ang.astype(np.float64)).astype(np.float32).T
    cos2 = np.concatenate([cos, cos, cos, cos], axis=0)
    sinS = np.concatenate([-sin, sin, -sin, sin], axis=0)
    return {"cos2": np.ascontiguousarray(cos2), "sinS": np.ascontiguousarray(sinS)}


W_SHAPES = {
    "a_w_in": [D, 6160], "a_b_gates": [16], "a_conv_w": [4, 2048], "a_conv_b": [2048], "a_norm_w": [2048], "a_w_out": [2048, 2048],
    "kv_w_down": [D, 576], "kv_norm_w": [512], "kv_w_up": [512, 4096], "b_w_dq": [D, 512], "b_q_norm_w": [512], "b_w_uq": [512, 3072],
    "b_w_out": [2048, 2048], "mlp_w1": [2, D, DFF], "mlp_w2": [2, DFF, D], "ln1_g": [2, D], "ln1_b": [2, D], "ln2_g": [2, D], "ln2_b": [2, D],
}


def all_consts():
    hc = dict(host_consts())
    hc.update(rope_tables())
    return hc


def build_full():
    nc = bass.Bass("TRN2", target_bir_lowering=False)

    def dt(n, sh, kind=None, d=F32):
        if kind is None:
            return nc.dram_tensor(n, sh, d).ap()
        return nc.dram_tensor(n, sh, d, kind=kind).ap()
    x = dt("x", [S, D], "ExternalInput")
    w = {k: dt(k, sh, "ExternalInput") for k, sh in W_SHAPES.items()}
    consts = {k: dt("c_" + k, list(v.shape), "ExternalInput") for k, v in all_consts().items()}
    out = dt("out", [S, D], "ExternalOutput")
    v_s = dt("v_s", [S, 2048], None, BF16)
    so_s = dt("so_s", [S, 2048])
    hg_d = dt("hg_d", [S, 2048], None, BF16)
    x1 = dt("x1", [S, D])
    x2 = dt("x2", [S, D])
    x3 = dt("x3", [S, D])
    knT_d = dt("knT_d", [16, 128, S], None, BF16)
    krT_d = dt("krT_d", [128, S], None, BF16)
    v_d = dt("v_d", [16, S, 128], None, BF16)
    qnT_d = dt("qnT_d", [16, 128, S], None, BF16)
    qrT_d = dt("qrT_d", [8, 128, S], None, BF16)
    oT_d = dt("oT_d", [2048, S], None, BF16)
    with contextlib.ExitStack() as es:
        C = Ctx(nc, es)
        load_consts(C, consts)
        xB, hgB, x1B, x2B, x3B, kvB, oTB, outB = [Buf() for _ in range(8)]
        phase_mlstm(C, x, xB, w["a_w_in"], w["a_b_gates"], w["a_conv_w"], w["a_conv_b"], w["a_norm_w"], consts, v_s, so_s, hg_d, hgB)
        phase_out(C, hg_d, hgB, False, w["a_w_out"], x, xB, w["ln1_g"][0], w["ln1_b"][0], x1, x1B)
        phase_mlp2(C, x1, w["mlp_w1"][0], w["mlp_w2"][0], w["ln2_g"][0], w["ln2_b"][0], x2, x1B, x2B)
        phase_kvq(C, x2, x2B, w["kv_w_down"], w["kv_norm_w"], w["kv_w_up"], w["b_w_dq"], w["b_q_norm_w"], w["b_w_uq"], consts,
                  knT_d, krT_d, v_d, qnT_d, qrT_d, kvB)
        phase_att(C, knT_d, krT_d, v_d, qnT_d, qrT_d, kvB, oT_d, oTB)
        phase_out(C, oT_d, oTB, True, w["b_w_out"], x2, x2B, w["ln1_g"][1], w["ln1_b"][1], x3, x3B)
        phase_mlp2(C, x3, w["mlp_w1"][1], w["mlp_w2"][1], w["ln2_g"][1], w["ln2_b"][1], out, x3B, outB)
        block = es.enter_context(nc.Block())
        C.P.emit(block, es)
    return nc


def kernel(**inputs):
    x = np.ascontiguousarray(np.asarray(inputs["x"], dtype=np.float32))
    consts = all_consts()
    shared = {}
    for k, sh in W_SHAPES.items():
        a = np.asarray(inputs[k], dtype=np.float32)
        shared[k] = np.ascontiguousarray(a.reshape(sh))
    for k, v in consts.items():
        shared["c_" + k] = v
    nc = build_full()
    in_maps = []
    for c in range(NCORES):
        m = dict(shared)
        m["x"] = x[c]
        in_maps.append(m)
    res = run_bass_kernel_spmd(nc, in_maps, core_ids=list(range(NCORES)))
    return np.stack([np.asarray(r["out"], dtype=np.float32) for r in res.results], axis=0)


def phase_mlp2(C, x_in, w1, w2, g, b, x_out, xinB, xoutB):
    P = C.P
    NS = 8
    with contextlib.ExitStack() as es:
        xs = C.sb([128, NS, D], F32, es=es)
        xsb = [Buf() for _ in range(NS)]
        xT = C.sb([128, 16, 1024], BF16, es=es)
        xTb = Buf()
        hT = C.sb([128, 16, 1024], BF16, es=es)
        hTb = [Buf() for _ in range(16)]
        ring = PanelRing(C, 2, es)
        xstg = C.sb([128, 2, D], F32, es=es)
        xstgB = [Buf() for _ in range(2)]
        rst = [C.sb([128, 512], F32, es=es) for _ in range(2)]
        rstB = [Buf() for _ in range(2)]
        gb = C.sb([128, D], F32, es=es)
        bb = C.sb([128, D], F32, es=es)
        gbB = Buf()
        bcast_load(C, gb, g, gbB)
        bcast_load(C, bb, b, gbB)
        st = ln_stat_sets(C, es)
        pb = [C.ps([128, 512], F32, es=es) for _ in range(8)]
        pbB = [Buf() for _ in range(8)]
        for stile in range(S // 1024):
            for q in range(4):
                ring.plan([panel_src(w1, 0, q * 2048 + fg * 512) for fg in range(4)])
                ring.plan([panel_src(w2, q * 2048, nt * 512) for nt in range(4)])
        k1 = 0
        k2 = 0
        kk_ = [0]

        def build_xT(stile):
            row0 = stile * 1024
            for sub in range(NS):
                sg, sgB = xstg[:, sub % 2, :], xstgB[sub % 2]
                P.dma("sp", lambda e, sub=sub, row0=row0, sg=sg: e.dma_start(out=sg, in_=x_in[row0 + sub * 128: row0 + (sub + 1) * 128, :]),
                      reads=[xinB], writes=[sgB])
                for gq in range(4):
                    k = kk_[0]
                    kk_[0] += 1
                    pt = pb[4 + k % 4][:].rearrange("p (a b) -> p a b", a=4)
                    ptb = pbB[4 + k % 4]
                    for j in range(4):
                        kc = gq * 4 + j
                        P.op("pe", lambda e, pt=pt, j=j, kc=kc, sg=sg: e.transpose(out=pt[:, j, :], in_=sg[:, kc * 128:(kc + 1) * 128], identity=C.ident[:]),
                             reads=[sgB, C.identB], writes=[ptb])
                    dst = xT[:, gq * 4:(gq + 1) * 4, sub * 128:(sub + 1) * 128]
                    if k % 2 == 0:
                        P.op("act", lambda e, pt=pt, dst=dst: e.activation(out=dst, in_=pt, func=AF.Copy), reads=[ptb], writes=[xTb])
                    else:
                        P.op("dve", lambda e, pt=pt, dst=dst: e.tensor_copy(out=dst, in_=pt), reads=[ptb], writes=[xTb])
        build_xT(0)
        for stile in range(S // 1024):
            row0 = stile * 1024
            for sub in range(NS):
                P.dma("sp", lambda e, sub=sub, row0=row0: e.dma_start(out=xs[:, sub, :], in_=x_in[row0 + sub * 128: row0 + (sub + 1) * 128, :]),
                      reads=[xinB], writes=[xsb[sub]])
            for q in range(4):
                for fg in range(4):
                    wt, wb = ring.next()
                    for fc in range(4):
                        f = fg * 4 + fc
                        for tk in range(2):
                            bank, bankB = pb[4 + k1 % 4], pbB[4 + k1 % 4]
                            for kc in range(16):
                                P.op("pe", lambda e, wt=wt, kc=kc, fc=fc, bank=bank, tk=tk:
                                     e.matmul(out=bank[:], lhsT=wt[:, kc * 512 + fc * 128: kc * 512 + (fc + 1) * 128], rhs=xT[:, kc, tk * 512:(tk + 1) * 512],
                                              start=(kc == 0), stop=(kc == 15)), reads=[wb, xTb], writes=[bankB])
                            r, rB = rst[k1 % 2], rstB[k1 % 2]
                            P.op("act", lambda e, r=r, bank=bank: e.activation(out=r[:], in_=bank[:], func=AF.Relu), reads=[bankB], writes=[rB])
                            P.op("act", lambda e, r=r, f=f, tk=tk: e.activation(out=hT[:, f, tk * 512:(tk + 1) * 512], in_=r[:], func=AF.Square),
                                 reads=[rB], writes=[hTb[f]])
                            k1 += 1
                if q == 3 and stile + 1 < S // 1024:
                    build_xT(stile + 1)
                for nt in range(4):
                    wt, wb = ring.next()
                    for sub in range(NS):
                        bank, bankB = pb[k2 % 4], pbB[k2 % 4]
                        k2 += 1
                        for fc in range(16):
                            P.op("pe", lambda e, wt=wt, fc=fc, sub=sub, bank=bank:
                                 e.matmul(out=bank[:], lhsT=hT[:, fc, sub * 128:(sub + 1) * 128], rhs=wt[:, fc * 512:(fc + 1) * 512], start=(fc == 0), stop=(fc == 15)),
                                 reads=[wb, hTb[fc]], writes=[bankB])
                        zs = xs[:, sub, nt * 512:(nt + 1) * 512]
                        if q == 0:
                            P.op("dve", lambda e, zs=zs, bank=bank: e.scalar_tensor_tensor(out=zs, in0=zs, scalar=DN_ALPHA, in1=bank[:], op0=ALU.mult, op1=ALU.add),
                                 reads=[bankB], writes=[xsb[sub]])
                        else:
                            P.op("dve", lambda e, zs=zs, bank=bank: e.tensor_tensor(out=zs, in0=bank[:], in1=zs, op=ALU.add), reads=[bankB], writes=[xsb[sub]])
            def store(sub, row0=row0):
                P.dma("sp", lambda e, sub=sub, row0=row0: e.dma_start(out=x_out[row0 + sub * 128: row0 + (sub + 1) * 128, :], in_=xs[:, sub, :]),
                      reads=[xsb[sub]], writes=[xoutB], par=True)
            layernorm_tiles(C, [xs[:, sub, :] for sub in range(NS)], xsb, gb, bb, gbB, st, store)
        P.barrier()
```
